# Optimizing a Trainium2 kernel written in Bass

```python
import jax
import jax.numpy as jnp
from jax import lax
import numpy as np

D_MODEL = 1024
BATCH = 32
SEQ = 2048
DEPTH = 1
DEC_BATCH = 16
DEC_SEQ = 4096
PAST_LEN = 128

GRID_W = 64
NA_HEADS = 8
NA_HEAD_DIM = D_MODEL // 16
NA_WIDTH = NA_HEADS * NA_HEAD_DIM
NA_KH_MAX = 8
NA_KW = 16
GLA_HEADS = 4
GLA_KEY_WIDTH = D_MODEL // 4
GLA_VAL_WIDTH = D_MODEL // 2
GLA_DK = GLA_KEY_WIDTH // GLA_HEADS
GLA_DV = GLA_VAL_WIDTH // GLA_HEADS
GLA_GATE_RANK = 16
GLA_GATE_NORMALIZER = 16.0
GLA_CHUNK = 16
D_FF = 4 * D_MODEL
RMS_EPS = 1e-6
IN_WIDTHS = (NA_WIDTH, NA_WIDTH, NA_WIDTH, GLA_KEY_WIDTH, GLA_KEY_WIDTH, GLA_VAL_WIDTH, GLA_VAL_WIDTH, GLA_GATE_RANK, GLA_GATE_RANK, D_MODEL, D_MODEL)
D_IN = sum(IN_WIDTHS)

kernel_name = 'hybrid_na_gla_encoder'


def rmsnorm(x, g):
    xf = x.astype(jnp.float32)
    xf = xf * lax.rsqrt(jnp.mean(xf * xf, axis=-1, keepdims=True) + RMS_EPS)
    return xf.astype(x.dtype) * g


def to_heads(t, n_heads):
    b, t_len, _ = t.shape
    return t.reshape(b, t_len, n_heads, -1).transpose(0, 2, 1, 3)


def from_heads(t):
    b, h, t_len, d = t.shape
    return t.transpose(0, 2, 1, 3).reshape(b, t_len, h * d)


def neighbourhood_attention(q, k, v, rpb):
    b, h, t_len, d = q.shape
    rows = t_len // GRID_W
    kh = min(NA_KH_MAX, rows)
    r = np.arange(rows)
    row_start = np.clip(r - kh // 2, 0, rows - kh)
    row_idx = row_start[:, None] + np.arange(kh)[None, :]
    dr_idx = row_idx - r[:, None] + (NA_KH_MAX - 1)
    c = np.arange(GRID_W)
    col_start = np.clip(c - NA_KW // 2, 0, GRID_W - NA_KW)
    col_mask = (c[None, :] >= col_start[:, None]) & (c[None, :] < col_start[:, None] + NA_KW)
    dc_idx = np.clip(c[None, :] - c[:, None], -(NA_KW - 1), NA_KW - 1) + (NA_KW - 1)
    qg = q.reshape(b, h, rows, GRID_W, d) * (d ** -0.5)
    kg = k.reshape(b, h, rows, GRID_W, d)[:, :, row_idx]
    vg = v.reshape(b, h, rows, GRID_W, d)[:, :, row_idx]
    bias = rpb[:, dr_idx[:, None, :, None], dc_idx[None, :, None, :]]
    s = jnp.einsum('bhrqd,bhrkcd->bhrqkc', qg, kg).astype(jnp.float32) + bias.astype(jnp.float32)
    s = jnp.where(col_mask[:, None, :], s, -jnp.inf)
    p = jax.nn.softmax(s.reshape(b, h, rows, GRID_W, kh * GRID_W), axis=-1)
    p = p.reshape(s.shape).astype(v.dtype)
    o = jnp.einsum('bhrqkc,bhrkcd->bhrqd', p, vg)
    return o.reshape(b, h, t_len, d)


def gla_chunked(q, k, v, log_a):
    b, h, t_len, dk = q.shape
    dv = v.shape[-1]
    n = t_len // GLA_CHUNK
    q, k, log_a = (t.astype(jnp.float32).reshape(b, h, n, GLA_CHUNK, dk) for t in (q, k, log_a))
    v = v.astype(jnp.float32).reshape(b, h, n, GLA_CHUNK, dv)
    cum = jnp.cumsum(log_a, axis=3)
    cum_last = cum[:, :, :, -1:, :]
    causal = np.tril(np.ones((GLA_CHUNK, GLA_CHUNK), dtype=bool))
    diff = cum[:, :, :, :, None, :] - cum[:, :, :, None, :, :]
    decay = jnp.exp(jnp.where(causal[:, :, None], diff, -jnp.inf))
    scores = jnp.einsum('bhnid,bhnjd,bhnijd->bhnij', q, k, decay)
    o_intra = jnp.einsum('bhnij,bhnjv->bhniv', scores, v)
    q_dec = q * jnp.exp(cum)
    k_dec = k * jnp.exp(cum_last - cum)
    chunk_decay = jnp.exp(cum_last[:, :, :, 0, :])

    def step(state, xs):
        qd, kd, vc, cd = xs
        o = jnp.einsum('bhid,bhdv->bhiv', qd, state)
        state = cd[..., None] * state + jnp.einsum('bhjd,bhjv->bhdv', kd, vc)
        return state, o

    xs = tuple(jnp.moveaxis(t, 2, 0) for t in (q_dec, k_dec, v, chunk_decay))
    state0 = jnp.zeros((b, h, dk, dv), jnp.float32)
    _, o_inter = lax.scan(step, state0, xs)
    return (o_intra + jnp.moveaxis(o_inter, 0, 2)).reshape(b, h, t_len, dv)


def gla_bidirectional(q, k, v, log_a_fwd, log_a_bwd):
    o_fwd = gla_chunked(q, k, v, log_a_fwd)
    flip = lambda t: jnp.flip(t, axis=2)
    o_bwd = flip(gla_chunked(flip(q), flip(k), flip(v), flip(log_a_bwd)))
    return o_fwd + o_bwd


def hybrid_mixer(u, w_in, b_in, na_rpb, gk_fwd_w, gk_fwd_b, gk_bwd_w, gk_bwd_b, gla_norm_g, w_br_na, w_br_gla, w_out):
    proj = jnp.einsum('btd,de->bte', u, w_in) + b_in
    split_points = np.cumsum(IN_WIDTHS)[:-1].tolist()
    (na_q, na_k, na_v, gla_q, gla_k, gla_v, gla_g, lr_fwd, lr_bwd, gate_na, gate_gla) = jnp.split(proj, split_points, axis=-1)
    na_o = neighbourhood_attention(to_heads(na_q, NA_HEADS), to_heads(na_k, NA_HEADS), to_heads(na_v, NA_HEADS), na_rpb)
    na_out = jnp.einsum('bte,ed->btd', from_heads(na_o), w_br_na)
    log_a_fwd = jax.nn.log_sigmoid((jnp.einsum('btr,rk->btk', lr_fwd, gk_fwd_w) + gk_fwd_b).astype(jnp.float32)) / GLA_GATE_NORMALIZER
    log_a_bwd = jax.nn.log_sigmoid((jnp.einsum('btr,rk->btk', lr_bwd, gk_bwd_w) + gk_bwd_b).astype(jnp.float32)) / GLA_GATE_NORMALIZER
    gla_o = gla_bidirectional(to_heads(gla_q, GLA_HEADS) * (GLA_DK ** -0.5), to_heads(gla_k, GLA_HEADS), to_heads(gla_v, GLA_HEADS), to_heads(log_a_fwd, GLA_HEADS), to_heads(log_a_bwd, GLA_HEADS))
    gla_o = rmsnorm(gla_o.astype(u.dtype), gla_norm_g)
    gla_o = from_heads(gla_o) * jax.nn.silu(gla_g)
    gla_out = jnp.einsum('bte,ed->btd', gla_o, w_br_gla)
    merged = jax.nn.sigmoid(gate_na) * na_out + jax.nn.sigmoid(gate_gla) * gla_out
    return jnp.einsum('btd,de->bte', merged, w_out)


def sq_relu_mlp(u, w_up, w_down):
    hdn = jnp.square(jax.nn.relu(jnp.einsum('btd,df->btf', u, w_up)))
    return jnp.einsum('btf,fd->btd', hdn, w_down)


def trunk(x, norm_mix_g, w_in, b_in, na_rpb, gk_fwd_w, gk_fwd_b, gk_bwd_w, gk_bwd_b, gla_norm_g, w_br_na, w_br_gla, w_out, norm_mlp_g, w_up, w_down, norm_final_g):
    h = x
    for l in range(DEPTH):
        h = h + hybrid_mixer(rmsnorm(h, norm_mix_g[l]), w_in[l], b_in[l], na_rpb[l], gk_fwd_w[l], gk_fwd_b[l], gk_bwd_w[l], gk_bwd_b[l], gla_norm_g[l], w_br_na[l], w_br_gla[l], w_out[l])
        h = h + sq_relu_mlp(rmsnorm(h, norm_mlp_g[l]), w_up[l], w_down[l])
    return rmsnorm(h, norm_final_g)


def setup_inputs(seed: int = 0) -> dict:
    key = jax.random.key(seed)
    ks = jax.random.split(key, 20)

    def nrm(k, shape, scale):
        return jax.random.normal(k, shape, jnp.float32) * scale

    return {
        'x_prompt': nrm(ks[0], (BATCH, SEQ, D_MODEL), 1.0),
        'x_sample': nrm(ks[1], (DEC_BATCH, DEC_SEQ, D_MODEL), 1.0),
        'norm_mix_g': 1.0 + nrm(ks[2], (DEPTH, D_MODEL), 0.01),
        'w_in': nrm(ks[3], (DEPTH, D_MODEL, D_IN), D_MODEL ** -0.5),
        'b_in': nrm(ks[4], (DEPTH, D_IN), 0.01),
        'na_rpb': nrm(ks[5], (DEPTH, NA_HEADS, 2 * NA_KH_MAX - 1, 2 * NA_KW - 1), 0.02),
        'gk_fwd_w': nrm(ks[6], (DEPTH, GLA_GATE_RANK, GLA_KEY_WIDTH), GLA_GATE_RANK ** -0.5),
        'gk_fwd_b': nrm(ks[7], (DEPTH, GLA_KEY_WIDTH), 0.01),
        'gk_bwd_w': nrm(ks[8], (DEPTH, GLA_GATE_RANK, GLA_KEY_WIDTH), GLA_GATE_RANK ** -0.5),
        'gk_bwd_b': nrm(ks[9], (DEPTH, GLA_KEY_WIDTH), 0.01),
        'gla_norm_g': 1.0 + nrm(ks[10], (DEPTH, GLA_DV), 0.01),
        'w_br_na': nrm(ks[11], (DEPTH, NA_WIDTH, D_MODEL), NA_WIDTH ** -0.5),
        'w_br_gla': nrm(ks[12], (DEPTH, GLA_VAL_WIDTH, D_MODEL), GLA_VAL_WIDTH ** -0.5),
        'w_out': nrm(ks[13], (DEPTH, D_MODEL, D_MODEL), D_MODEL ** -0.5),
        'norm_mlp_g': 1.0 + nrm(ks[14], (DEPTH, D_MODEL), 0.01),
        'w_up': nrm(ks[15], (DEPTH, D_MODEL, D_FF), D_MODEL ** -0.5),
        'w_down': nrm(ks[16], (DEPTH, D_FF, D_MODEL), D_FF ** -0.5),
        'norm_final_g': 1.0 + nrm(ks[17], (D_MODEL,), 0.01),
    }


def reference(x_prompt, x_sample, norm_mix_g, w_in, b_in, na_rpb, gk_fwd_w, gk_fwd_b, gk_bwd_w, gk_bwd_b, gla_norm_g, w_br_na, w_br_gla, w_out, norm_mlp_g, w_up, w_down, norm_final_g):
    y_prompt = trunk(x_prompt, norm_mix_g, w_in, b_in, na_rpb, gk_fwd_w, gk_fwd_b, gk_bwd_w, gk_bwd_b, gla_norm_g, w_br_na, w_br_gla, w_out, norm_mlp_g, w_up, w_down, norm_final_g)
    y_sample = trunk(x_sample, norm_mix_g, w_in, b_in, na_rpb, gk_fwd_w, gk_fwd_b, gk_bwd_w, gk_bwd_b, gla_norm_g, w_br_na, w_br_gla, w_out, norm_mlp_g, w_up, w_down, norm_final_g)
    return (y_prompt, y_sample)
```

```python
import os
import numpy as np
from contextlib import ExitStack
import concourse.bass as bass
import concourse.mybir as mybir
from concourse.bass_utils import run_bass_kernel_spmd

F32 = mybir.dt.float32
BF16 = mybir.dt.bfloat16
AF = mybir.ActivationFunctionType
ALU = mybir.AluOpType
AX = mybir.AxisListType

D = 1024
NCORES = 8
NB = 31
BLK = 4096
RING = int(os.environ.get('K_RING', '4'))
EPS = 1e-6
NA_INTERLEAVE = os.environ.get('K_IL', '0') == '1'
KVS = int(os.environ.get('K_KVS', '12'))
C_NAQ, C_NAK, C_NAV = (0, 512), (512, 1024), (1024, 1536)
C_GQ, C_GK, C_GV, C_GG = (1536, 1792), (1792, 2048), (2048, 2560), (2560, 3072)
C_LRF, C_LRB = (3072, 3088), (3088, 3104)
C_GNA, C_GGLA = (3104, 4128), (4128, 5152)
EB_ORDER = [5, 4, 0, 1, 6, 2, 3, 7, 8]
NEB = len(EB_ORDER)


class View:
    __slots__ = ("buf", "ap", "lo", "hi")

    def __init__(self, buf, ap, lo, hi):
        self.buf, self.ap, self.lo, self.hi = buf, ap, lo, hi

    def re(self, pat, **kw):
        return View(self.buf, self.ap.rearrange(pat, **kw), self.lo, self.hi)

    def bc(self, axis, shape):
        return View(self.buf, self.ap.unsqueeze(axis).to_broadcast(list(shape)), self.lo, self.hi)

    def sub(self, key):
        return View(self.buf, self.ap[key], self.lo, self.hi)


class Buf:
    def __init__(self, name, t, shape, space, parent=None, base=0):
        self.name, self.t, self.shape, self.space = name, t, tuple(shape), space
        self.parent, self.base = parent, base
        self.bank_elems = 512
        n = len(shape)
        fs = [0] * n
        acc = 1
        for i in range(n - 1, 0, -1):
            fs[i] = acc
            acc *= shape[i]
        fs[0] = acc if space == "dram" else 0
        self.fs = fs
        self.wrecs = []
        self.rrecs = []

    @property
    def root(self):
        return self.parent if self.parent is not None else self

    def __getitem__(self, key):
        if not isinstance(key, tuple):
            key = (key,)
        key = key + (slice(None),) * (len(self.shape) - len(key))
        lo = hi = self.base
        for d, k in enumerate(key):
            n = self.shape[d]
            if isinstance(k, int):
                a = b = k
            else:
                a = 0 if k.start is None else k.start
                e = n if k.stop is None else k.stop
                st = 1 if k.step is None else k.step
                b = a + ((e - a - 1) // st) * st
            lo += a * self.fs[d]
            hi += b * self.fs[d]
        hi += 1
        if self.space == "psum":
            be = self.bank_elems
            lo = (lo // be) * be
            hi = -(-hi // be) * be
        return View(self, self.t[key], lo, hi)


class Op:
    __slots__ = ("eng", "fn", "deps", "sig", "sigval", "dma", "chan", "chanval", "idx", "dur", "lat", "fin", "nin", "succ")


class Prog:
    ENGS = ("pe", "act", "dve", "pool", "sp")

    def __init__(self):
        self.ops = []
        self.chan_count = {}

    def add(self, eng, fn, reads, writes, chan=None, dur=100.0, lat=None):
        op = Op()
        op.eng, op.fn, op.sig, op.sigval = eng, fn, False, 0
        op.dma = chan is not None
        op.chan = chan
        op.idx = len(self.ops)
        op.dur = dur
        op.lat = dur if lat is None else lat
        if op.dma:
            self.chan_count[chan] = self.chan_count.get(chan, 0) + 16
            op.chanval = self.chan_count[chan]
        deps = set()
        for v in reads:
            for r in v.buf.root.wrecs:
                if r[0] < v.hi and v.lo < r[1]:
                    deps.add(r[2])
        for v in writes:
            b = v.buf.root
            for r in b.wrecs:
                if r[0] < v.hi and v.lo < r[1]:
                    deps.add(r[2])
            for r in b.rrecs:
                if r[0] < v.hi and v.lo < r[1]:
                    deps.add(r[2])
        deps.discard(op)
        op.deps = deps
        for v in reads:
            v.buf.root.rrecs.append((v.lo, v.hi, op))
        for v in writes:
            b = v.buf.root
            b.wrecs = [r for r in b.wrecs if not (r[0] >= v.lo and r[1] <= v.hi)]
            b.rrecs = [r for r in b.rrecs if not (r[0] >= v.lo and r[1] <= v.hi)]
            b.wrecs.append((v.lo, v.hi, op))
        self.ops.append(op)
        return op

    def schedule(self):
        import heapq
        ops = self.ops
        for op in ops:
            op.succ = []
            op.nin = len(op.deps)
            op.fin = 0.0
        for op in ops:
            for d in op.deps:
                d.succ.append(op)
        prio_mode = os.environ.get("K_PRIO", "0.5")
        if prio_mode != "idx":
            bl = [0.0] * len(ops)
            for op in reversed(ops):
                m_ = 0.0
                for s_ in op.succ:
                    if bl[s_.idx] > m_:
                        m_ = bl[s_.idx]
                bl[op.idx] = m_ + op.lat
            tot = bl[0] if bl else 1.0
            w_ = float(prio_mode)
            scale = len(ops) / max(tot, 1.0)
            key_of = [op.idx - w_ * (bl[op.idx] * scale - (len(ops) - op.idx)) for op in ops]
        else:
            key_of = [float(op.idx) for op in ops]
        pending = {e: [] for e in self.ENGS}
        avail = {e: [] for e in self.ENGS}
        free = {e: 0.0 for e in self.ENGS}
        ready_t = {}
        for op in ops:
            if op.nin == 0:
                heapq.heappush(pending[op.eng], (0.0, op.idx))
        order = {e: [] for e in self.ENGS}
        left = len(ops)
        XLAT = 80.0
        while left:
            best = None
            for e in self.ENGS:
                pe_, av = pending[e], avail[e]
                t = free[e]
                while pe_ and pe_[0][0] <= t:
                    i_ = heapq.heappop(pe_)[1]
                    heapq.heappush(av, (key_of[i_], i_))
                if av:
                    cand = (t, av[0][1], e, True)
                elif pe_:
                    cand = (pe_[0][0], pe_[0][1], e, False)
                else:
                    continue
                if best is None or cand[:2] < best[:2]:
                    best = cand
            start, idx, e, from_av = best
            if from_av:
                heapq.heappop(avail[e])
            else:
                heapq.heappop(pending[e])
            op = ops[idx]
            order[e].append(op)
            free[e] = start + op.dur
            op.fin = start + op.lat
            left -= 1
            for s_ in op.succ:
                s_.nin -= 1
                rt = op.fin + (0.0 if (s_.eng == op.eng and not op.dma) else XLAT)
                if ready_t.get(s_.idx, 0.0) < rt:
                    ready_t[s_.idx] = rt
                if s_.nin == 0:
                    heapq.heappush(pending[s_.eng], (ready_t.get(s_.idx, 0.0), s_.idx))
        self.sim_time = max(op.fin for op in ops)
        self.sim_busy = {e: sum(op.dur for op in order[e]) for e in self.ENGS}
        return order

    def emit(self, nc, es):
        per = self.schedule()
        for op in self.ops:
            for d in op.deps:
                if (not d.dma) and not (d.eng == "pe" and op.eng == "pe"):
                    d.sig = True
        for e in self.ENGS:
            c = 0
            for op in per[e]:
                if (not op.dma) and op.sig:
                    c += 1
                    op.sigval = c
        esem = {e: es.enter_context(nc.semaphore("s_" + e)) for e in self.ENGS}
        csem = {c: es.enter_context(nc.semaphore("c_" + c)) for c in self.chan_count}
        handles = {"pe": "tensor", "act": "scalar", "dve": "vector", "pool": "gpsimd", "sp": "sync"}
        final_waits = [(csem[c], v) for c, v in self.chan_count.items()]

        def run(e, h):
            waited = {}
            for op in per[e]:
                need = {}
                for d in op.deps:
                    if d.dma:
                        k, val = csem[d.chan], d.chanval
                    elif d.eng == "pe" and e == "pe":
                        continue
                    else:
                        k, val = esem[d.eng], d.sigval
                    if need.get(k, 0) < val:
                        need[k] = val
                for k, val in need.items():
                    if waited.get(k, 0) < val:
                        h.wait_ge(k, val)
                        waited[k] = val
                ins = op.fn(h)
                if op.dma:
                    ins.then_inc(csem[op.chan], 16)
                elif op.sig:
                    ins.then_inc(esem[e], 1)
            if e == "sp":
                for k, val in final_waits:
                    if waited.get(k, 0) < val:
                        h.wait_ge(k, val)

        block = es.enter_context(nc.Block())
        block.tensor(lambda h: run("pe", h))
        block.scalar(lambda h: run("act", h))
        block.vector(lambda h: run("dve", h))
        block.gpsimd(lambda h: run("pool", h))
        block.sync(lambda h: run("sp", h))


def _blk(w, cols, nk, ncols):
    out = np.zeros((128, nk, ncols), np.float32)
    sub = w[:, cols].reshape(nk, 128, len(cols))
    out[:, :, : len(cols)] = sub.transpose(1, 0, 2)
    return out.reshape(128, nk * ncols)


def _na_variants():
    rows, kh = 64, 8
    rs = lambda r: int(np.clip(r - 4, 0, rows - kh))
    allv = {}
    for m in range(rows // 2):
        lo, hi = rs(2 * m), rs(2 * m + 1) + 8
        for kt in range(lo // 2, (hi - 1) // 2 + 1):
            key = []
            for krl in range(2):
                for qrl in range(2):
                    kr, qr = 2 * kt + krl, 2 * m + qrl
                    key.append(kr - qr + 7 if rs(qr) <= kr < rs(qr) + 8 else None)
            allv.setdefault(tuple(key), len(allv))
    return allv


def _na_tile_info(rows):
    kh = 8
    rs = lambda r: int(np.clip(r - 4, 0, rows - kh))
    allv = _na_variants()
    info = []
    for m in range(rows // 2):
        lo, hi = rs(2 * m), rs(2 * m + 1) + 8
        kts = list(range(lo // 2, (hi - 1) // 2 + 1))
        pat = []
        for kt in kts:
            key = []
            for krl in range(2):
                for qrl in range(2):
                    kr, qr = 2 * kt + krl, 2 * m + qrl
                    key.append(kr - qr + 7 if rs(qr) <= kr < rs(qr) + 8 else None)
            pat.append(allv[tuple(key)])
        pos = [EB_ORDER.index(v) for v in pat]
        runs = []
        a = 0
        for b in range(1, len(pos) + 1):
            if b == len(pos) or pos[b] != pos[b - 1] + 1:
                runs.append((a, b - a, pos[a]))
                a = b
        info.append((kts, runs))
    return info


def _na_bias_table(rpb):
    allv = _na_variants()
    inv = {v: k for k, v in allv.items()}
    c = np.arange(64)
    col_start = np.clip(c - 8, 0, 48)
    cmask = (c[None, :] >= col_start[:, None]) & (c[None, :] < col_start[:, None] + 16)
    dc = np.clip(c[None, :] - c[:, None], -15, 15) + 15
    tab = np.full((128, NEB, 8, 128), -30000.0, np.float32)
    for e, vid in enumerate(EB_ORDER):
        key = inv[vid]
        for krl in range(2):
            for qrl in range(2):
                dr = key[krl * 2 + qrl]
                if dr is None:
                    continue
                vals = rpb[:, dr, :][:, dc]
                vals = np.where(cmask[None], vals, np.float32(-30000.0))
                tab[krl * 64:(krl + 1) * 64, e, :, qrl * 64:(qrl + 1) * 64] = vals.transpose(2, 0, 1)
    return tab.reshape(128, NEB * 8 * 128)


def _host_consts(inp):
    w_in = np.asarray(inp["w_in"][0], np.float32)
    b_in = np.asarray(inp["b_in"][0], np.float32)
    ar = lambda r: np.arange(r[0], r[1])
    blocks = []
    blocks.append(_blk(w_in, ar(C_NAQ), 8, 512))
    blocks.append(_blk(w_in, ar(C_NAK), 8, 512))
    blocks.append(_blk(w_in, ar(C_NAV), 8, 512))
    blocks.append(_blk(w_in, np.concatenate([ar(C_GQ), ar(C_GK)]), 8, 512))
    blocks.append(_blk(w_in, ar(C_GV), 8, 512))
    blocks.append(_blk(w_in, ar(C_GG), 8, 512))
    blocks.append(_blk(w_in, np.concatenate([ar(C_GK), ar(C_LRF), ar(C_LRB)]), 8, 512))
    blocks.append(_blk(w_in, ar(C_GNA)[:512], 8, 512))
    blocks.append(_blk(w_in, ar(C_GNA)[512:], 8, 512))
    blocks.append(_blk(w_in, ar(C_GGLA)[:512], 8, 512))
    blocks.append(_blk(w_in, ar(C_GGLA)[512:], 8, 512))
    blocks.append(_blk(np.asarray(inp["w_br_na"][0], np.float32), np.arange(1024), 4, 1024))
    blocks.append(_blk(np.asarray(inp["w_br_gla"][0], np.float32), np.arange(1024), 4, 1024))
    w_out = np.asarray(inp["w_out"][0], np.float32)
    blocks.append(_blk(w_out, np.arange(0, 512), 8, 512))
    blocks.append(_blk(w_out, np.arange(512, 1024), 8, 512))
    w_up = np.asarray(inp["w_up"][0], np.float32)
    for j in range(8):
        blocks.append(_blk(w_up, np.arange(512 * j, 512 * (j + 1)), 8, 512))
    w_down = np.asarray(inp["w_down"][0], np.float32)
    for j in range(8):
        blocks.append(_blk(w_down[512 * j:512 * (j + 1)], np.arange(1024), 4, 1024))
    wsrc = np.stack(blocks, 0)
    assert wsrc.shape == (NB, 128, BLK)

    bfm = np.zeros((128, 40), np.float32)
    for ch in range(4):
        bfm[:, ch] = b_in[C_NAQ[0] + ch * 128: C_NAQ[0] + (ch + 1) * 128]
        bfm[:, 4 + ch] = b_in[C_NAK[0] + ch * 128: C_NAK[0] + (ch + 1) * 128]
    for h in range(4):
        bfm[:64, 8 + h] = b_in[C_GQ[0] + h * 64: C_GQ[0] + (h + 1) * 64]
        bfm[:64, 12 + h] = b_in[C_GK[0] + h * 64: C_GK[0] + (h + 1) * 64]
    bfm[:16, 16] = b_in[C_LRF[0]:C_LRF[1]]
    bfm[16:32, 16] = b_in[C_LRB[0]:C_LRB[1]]
    for ch in range(8):
        bfm[:, 17 + ch] = b_in[C_GNA[0] + ch * 128: C_GNA[0] + (ch + 1) * 128]
        bfm[:, 25 + ch] = b_in[C_GGLA[0] + ch * 128: C_GGLA[0] + (ch + 1) * 128]
    brow = np.zeros((33, 1024), np.float32)
    brow[0, 0:512] = b_in[C_NAV[0]:C_NAV[1]]
    brow[0, 512:1024] = b_in[C_GV[0]:C_GV[1]]
    brow[32, 0:512] = b_in[C_GG[0]:C_GG[1]]
    brow[32, 512:768] = b_in[C_GK[0]:C_GK[1]]
    gk = np.zeros((33, 512), np.float32)
    gk[0:16, 0:256] = np.asarray(inp["gk_fwd_w"][0], np.float32)
    gk[16:32, 256:512] = np.asarray(inp["gk_bwd_w"][0], np.float32)
    gk[32, 0:256] = np.asarray(inp["gk_fwd_b"][0], np.float32)
    gk[32, 256:512] = np.asarray(inp["gk_bwd_b"][0], np.float32)
    gfm = np.zeros((128, 16), np.float32)
    gfm[:, 0:8] = np.asarray(inp["norm_mix_g"][0], np.float32).reshape(8, 128).T
    gfm[:, 8:16] = np.asarray(inp["norm_mlp_g"][0], np.float32).reshape(8, 128).T
    gfin = np.broadcast_to(np.asarray(inp["norm_final_g"], np.float32)[None, :], (128, 1024)).copy()
    ggla = np.broadcast_to(np.asarray(inp["gla_norm_g"][0], np.float32)[None, :], (128, 128)).copy()
    j = np.arange(128)[:, None]
    t = np.arange(128)[None, :]
    s = np.float32(-1.0 / 16.0)
    tri = np.concatenate([(j <= t) * s, (j > t) * s, (j >= t) * s, (j < t) * s], 1).astype(np.float32)
    c16 = np.concatenate([np.eye(128), (j <= t) * 1.0, (j >= t) * 1.0, np.ones((128, 128))], 1).astype(np.float32)
    natab = _na_bias_table(np.asarray(inp["na_rpb"][0], np.float32))
    return dict(wsrc=wsrc, bfm=bfm, brow=brow, gk=gk, gfm=gfm, gfin=gfin, ggla=ggla, tri=tri, c16=c16, natab=natab)


def build_program(seq_lens):
    ntok = int(sum(seq_lens))
    nc = bass.Bass("TRN2", target_bir_lowering=False)
    es = ExitStack()
    pr = Prog()

    def dram(name, shape, dt, kind):
        return Buf(name, nc.dram_tensor(name, list(shape), dt, kind=kind).ap(), shape, "dram")

    x_all = dram("x_all", [ntok, D], F32, "ExternalInput")
    y_all = dram("y_all", [ntok, D], F32, "ExternalOutput")
    wsrc = dram("wsrc", [NB, 128, BLK], F32, "ExternalInput")
    d_bfm = dram("bfm", [128, 40], F32, "ExternalInput")
    d_brow = dram("brow", [33, 1024], F32, "ExternalInput")
    d_gk = dram("gk", [33, 512], F32, "ExternalInput")
    d_gfm = dram("gfm", [128, 16], F32, "ExternalInput")
    d_gfin = dram("gfin", [128, 1024], F32, "ExternalInput")
    d_ggla = dram("ggla", [128, 128], F32, "ExternalInput")
    d_tri = dram("tri", [128, 512], F32, "ExternalInput")
    d_c16 = dram("c16", [128, 512], F32, "ExternalInput")
    d_natab = dram("natab", [128, NEB * 1024], F32, "ExternalInput")
    wbf = dram("wbf", [NB, 128, BLK], BF16, "Internal")
    rst = dram("rst", [2, 32, 64, 512], BF16, "Internal")
    d_uT = dram("d_uT", [2, 8, 128, 4096], BF16, "Internal")
    d_vtm = dram("d_vtm", [2, 8, 128, 2048], BF16, "Internal")
    d_ktm = dram("d_ktm", [2, 8, 128, 1024], BF16, "Internal")
    d_lrT = dram("d_lrT", [2, 8, 32, 512], BF16, "Internal")

    def sb(name, shape, dt):
        return Buf(name, es.enter_context(nc.sbuf_tensor("sb_" + name, list(shape), dt)), shape, "sbuf")

    def ps(name, shape, dt):
        return Buf(name, es.enter_context(nc.psum_tensor("ps_" + name, list(shape), dt)), shape, "psum")

    wring = [sb(f"wr{i}", [128, BLK], BF16) for i in range(RING)]
    c16 = sb("c16", [128, 512], BF16)
    tri = sb("tri", [128, 512], F32)
    bfm = sb("bfm", [128, 40], F32)
    hbfm = sb("hbfm", [128, 16], F32)
    brow = sb("brow", [33, 1024], BF16)
    gkaug = sb("gkaug", [33, 512], BF16)
    gfm = sb("gfm", [128, 16], F32)
    gfin = sb("gfin", [128, 1024], F32)
    gglah = sb("gglah", [128, 128], F32)
    EB = sb("EB", [128, NEB, 8, 128], BF16)
    xst = sb("xst", [128, 1, 1024], F32)
    hn = sb("hn", [128, 1, 1024], BF16)
    stat = sb("stat", [128, 128], F32)
    uT = sb("uT", [128, 2, 8, 512], BF16)
    hbuf = sb("hbuf", [128, 4, 1024], F32)
    kT = sb("kT", [128, KVS, 4, 128], BF16)
    vr = sb("vr", [128, KVS, 8, 65], BF16)
    qT = sb("qT", [128, 4, 512], BF16)
    ar = sb("ar", [128, 4096], BF16)
    gqT = Buf("gqT", ar.t[0:64, 0:2048].rearrange("p (h t) -> p h t", h=4), [64, 4, 512], "sbuf", parent=ar, base=0)
    gkT = Buf("gkT", ar.t[0:64, 2048:4096].rearrange("p (h t) -> p h t", h=4), [64, 4, 512], "sbuf", parent=ar, base=2048)
    mergedT = Buf("mergedT", ar.t[:, :].rearrange("p (k t) -> p k t", k=8), [128, 8, 512], "sbuf", parent=ar, base=0)
    hid = Buf("hid", ar.t[:, :].rearrange("p (a k t) -> p a k t", a=1, k=8), [128, 1, 8, 512], "sbuf", parent=ar, base=0)
    v_tm = sb("v_tm", [128, 4, 512], BF16)
    sg = sb("sg", [128, 4, 512], BF16)
    k_tm = sb("k_tm", [128, 4, 256], BF16)
    lrT = sb("lrT", [33, 512], BF16)
    uTA = sb("uTA", [128, 1, 8, 512], BF16)
    vA = sb("vA", [128, 2, 512], BF16)
    kA = sb("kA", [128, 2, 256], BF16)
    lrTA = sb("lrTA", [33, 512], BF16)
    spA = sb("spA", [128, 256], F32)
    GexpA = sb("GexpA", [128, 256], F32)
    kddA = sb("kddA", [128, 256], BF16)
    tmpf = [sb(f"tmpf{i}", [128, 512], F32) for i in range(4)]
    e_t = tmpf[3]
    sp_t = sb("sp_t", [128, 1, 512], F32)
    E1f = sb("E1f", [64, 512], F32)
    E2f = sb("E2f", [64, 512], F32)
    E1b = E1f
    E2b = E2f
    Dfw = sb("Dfw", [64, 4], F32)
    Gexp = sb("Gexp", [128, 256], F32)
    Db = sb("Db", [64, 8], F32)
    qdf = sb("qdf", [64, 4, 128], BF16)
    kdf = sb("kdf", [64, 4, 128], BF16)
    qdb = sb("qdb", [64, 4, 128], BF16)
    kdb = sb("kdb", [64, 4, 128], BF16)
    kdd = sb("kdd", [128, 256], BF16)
    Amf = sb("Amf", [128, 4, 128], BF16)
    Amb = sb("Amb", [128, 4, 128], BF16)
    go = sb("go", [128, 512], BF16)
    S = sb("S", [64, 512], F32)
    Sbf = sb("Sbf", [64, 512], BF16)
    R = sb("R", [64, 512], F32)
    Rbf = sb("Rbf", [64, 1, 512], BF16)
    Rin = sb("Rin", [64, 1, 512], BF16)
    pexp = sb("pexp", [128, 2, 640], BF16)
    PTb = sb("PTb", [128, 2, 5, 128], BF16)
    nao = sb("nao", [128, 512], BF16)
    naoT = sb("naoT", [128, 4, 512], BF16)
    glaoT = sb("glaoT", [128, 4, 512], BF16)
    tnh = sb("tnh", [128, 2, 512], BF16)
    rl = sb("rl", [128, 1, 512], BF16)
    NPG = 5
    pg = [ps(f"pg{i}", [128, 512], F32) for i in range(NPG)]
    pS = [ps(f"pS{i}", [128, 512], F32) for i in range(2)]
    pT = ps("pT", [128, 8, 128], BF16)
    pT.bank_elems = 1024

    try:
        print("[build] sbuf bytes remaining per partition:", nc.sbuf_bytes_remaining // 128 if nc.sbuf_bytes_remaining > 300000 else nc.sbuf_bytes_remaining, flush=True)
    except Exception as ex:
        print("[build] sbuf remaining n/a", ex)
    cnt = {"va": 0, "pg": 0, "stat": 0, "xst": 0, "tmpf": 0, "pS": 0, "na": 0, "rb": 0, "ri": 0, "rl": 0, "tn": 0}

    def nxt(key, n):
        v = cnt[key]
        cnt[key] = v + 1
        return v % n

    pool_mode = {"A": False}

    pfree_list = list(range(NPG))
    plive = set()

    def gps(kind="d"):
        assert pfree_list, "out of PSUM banks (missing pfree?)"
        b_ = pfree_list.pop(0)
        plive.add(b_)
        return pg[b_]

    def pfree(x):
        b_ = x.buf if isinstance(x, View) else x
        i_ = pg.index(b_)
        assert i_ in plive, i_
        plive.discard(i_)
        pfree_list.append(i_)

    def stat_cols(n=4):
        c = nxt("stat", 16) * 8
        return stat[:, c:c + n]

    ident = c16[:, 0:128]
    Mf = c16[:, 128:256]
    Mb = c16[:, 256:384]
    ones_row = c16[0:1, 384:512]
    UTs, SLTs, LTs, SUTs = tri[:, 0:128], tri[:, 128:256], tri[:, 256:384], tri[:, 384:512]

    def nfree(v):
        n = 1
        for d_ in v.ap.shape[1:]:
            n *= int(d_)
        return n

    def is16(v):
        return v.ap.dtype == BF16 and v.buf.space == "sbuf"

    def vcost(eng, out, ins):
        n = nfree(out)
        if eng == "pool":
            return 150.0 + 1.7 * n
        f = 0.55 if (is16(out) and all(is16(i) for i in ins)) else 1.04
        return 150.0 + f * n

    def mm(out, lhsT, rhs, start, stop):
        n = nfree(rhs) * (4 if rhs.ap.dtype == F32 else 1)
        dur = n / 2.05 + (85.0 if n < 256 else 5.0)
        pr.add("pe", lambda e, o=out.ap, l=lhsT.ap, r=rhs.ap, s0=start, s1=stop:
               e.matmul(o, lhsT=l, rhs=r, start=s0, stop=s1), [lhsT, rhs], [out], dur=dur, lat=dur + 120.0)

    def tr(out, in_):
        pr.add("pe", lambda e, o=out.ap, i=in_.ap, d=ident.ap: e.transpose(o, i, d), [in_, ident], [out],
               dur=160.0, lat=300.0)

    def act(out, in_, func, bias=None, scale=1.0, accum=None):
        reads = [in_]
        kw = {}
        if isinstance(bias, View):
            reads.append(bias)
            kw["bias"] = bias.ap
        elif bias is not None:
            kw["bias"] = float(bias)
        if isinstance(scale, View):
            reads.append(scale)
            kw["scale"] = scale.ap
        else:
            kw["scale"] = float(scale)
        writes = [out]
        if accum is not None:
            writes.append(accum)
            kw["accum_out"] = accum.ap
        d_ = 200.0 + 0.83 * nfree(out)
        pr.add("act", lambda e, o=out.ap, i=in_.ap, f=func, kw=kw: e.activation(out=o, in_=i, func=f, **kw),
               reads, writes, dur=d_, lat=d_ + 60.0)

    def _h(e, name):
        return getattr(e, name)

    def tt(eng, out, in0, in1, op):
        d_ = vcost(eng, out, [in0, in1])
        pr.add(eng, lambda e, o=out.ap, a=in0.ap, b=in1.ap, op=op: e.tensor_tensor(o, a, b, op), [in0, in1], [out],
               dur=d_, lat=d_ + 60.0)

    def ts(eng, out, in0, s1, s2, op0, op1):
        reads = [in0]
        a1 = s1.ap if isinstance(s1, View) else float(s1)
        a2 = s2.ap if isinstance(s2, View) else float(s2)
        reads += [s for s in (s1, s2) if isinstance(s, View)]
        d_ = vcost(eng, out, [in0])
        pr.add(eng, lambda e, o=out.ap, a=in0.ap, a1=a1, a2=a2, op0=op0, op1=op1:
               e.tensor_scalar(o, a, a1, a2, op0, op1), reads, [out], dur=d_, lat=d_ + 60.0)

    def tsm(eng, out, in0, s1):
        reads = [in0] + ([s1] if isinstance(s1, View) else [])
        a1 = s1.ap if isinstance(s1, View) else float(s1)
        d_ = vcost(eng, out, [in0])
        pr.add(eng, lambda e, o=out.ap, a=in0.ap, a1=a1: e.tensor_scalar_mul(o, a, a1), reads, [out], dur=d_, lat=d_ + 60.0)

    def stt(eng, out, in0, sc, in1, op0, op1):
        reads = [in0, in1] + ([sc] if isinstance(sc, View) else [])
        a = sc.ap if isinstance(sc, View) else float(sc)
        d_ = vcost(eng, out, [in0, in1])
        pr.add(eng, lambda e, o=out.ap, i0=in0.ap, a=a, i1=in1.ap, op0=op0, op1=op1:
               e.scalar_tensor_tensor(o, i0, a, i1, op0, op1), reads, [out], dur=d_, lat=d_ + 60.0)

    def cp(eng, out, in_):
        if eng == "act":
            d_ = 200.0 + 0.83 * nfree(out)
            pr.add("act", lambda e, o=out.ap, i=in_.ap: e.copy(o, i), [in_], [out], dur=d_, lat=d_ + 60.0)
        else:
            d_ = vcost(eng, out, [in_])
            pr.add(eng, lambda e, o=out.ap, i=in_.ap: e.tensor_copy(o, i), [in_], [out], dur=d_, lat=d_ + 60.0)

    def red(out, in_, op=ALU.add):
        d_ = 150.0 + 1.04 * nfree(in_)
        pr.add("dve", lambda e, o=out.ap, i=in_.ap, op=op: e.tensor_reduce(o, i, AX.X, op), [in_], [out], dur=d_, lat=d_ + 60.0)

    def recip(out, in_):
        pr.add("dve", lambda e, o=out.ap, i=in_.ap: e.reciprocal(o, i), [in_], [out], dur=200.0, lat=260.0)

    def memset(eng, v, val):
        pr.add(eng, lambda e, o=v.ap, val=val: e.memset(o, val), [], [v], dur=150.0 + nfree(v), lat=200.0 + nfree(v))

    def dma(q, out, in_, chan):
        nb = 1
        for d_ in out.ap.shape:
            nb *= int(d_)
        nb *= 2 if out.ap.dtype == BF16 else 4
        pr.add(q, lambda e, o=out.ap, i=in_.ap: e.dma_start(out=o, in_=i), [in_], [out], chan=chan,
               dur=60.0, lat=2200.0 + nb / 150.0)

    for i in range(0, NB, 4):
        j = min(NB, i + 4)
        dma("pool", wbf[i:j], wsrc[i:j], f"wc{i}")
    dma("pool", c16[:, :], d_c16[:, :], "k0")
    dma("sp", tri[:, :], d_tri[:, :], "k1")
    dma("sp", bfm[:, :], d_bfm[:, :], "k2")
    dma("pool", brow[:, :], d_brow[:, :], "k3")
    dma("pool", gkaug[:, :], d_gk[:, :], "k4")
    dma("sp", gfm[:, :], d_gfm[:, :], "k5")
    dma("sp", gfin[:, :], d_gfin[:, :], "k6")
    dma("sp", gglah[:, :], d_ggla[:, :], "k7")
    ts("dve", gglah[:, :], gglah[:, :], 0.5, 0.0, ALU.mult, ALU.add)
    ts("dve", hbfm[:, :], bfm[:, 17:33], 0.5, 0.0, ALU.mult, ALU.add)
    memset("pool", vr[:, :, :, :], 1.0)
    memset("pool", lrT[:, :], 1.0)
    memset("pool", lrTA[:, :], 1.0)
    for e in range(NEB):
        t0 = tmpf[e % 2]
        t1 = tmpf[2]
        for hf in range(2):
            dma("sp", t0[:, :], d_natab[:, e * 1024 + hf * 512: e * 1024 + (hf + 1) * 512], f"nt{e % 2}")
            act(t1[:, :], t0[:, :], AF.Exp)
            cp("dve", EB[:, e, hf * 4:(hf + 1) * 4, :], t1[:, :].re("p (h q) -> p h q", h=4))

    def step_blocks(mode):
        if mode == "A":
            return [4, 6]
        return [1, 2, 0, 3, 5, 7, 9, 11, 12, 8, 10, 13, 14,
                15, 16, 23, 24, 17, 18, 25, 26, 19, 20, 27, 28, 21, 22, 29, 30]

    steps = []
    t0 = 0
    seqs_ = []
    for si, T in enumerate(seq_lens):
        seqs_.append((t0, T, si % 2))
        t0 += T
    def a_steps(q):
        return [("A", q[0], q[1], g, q[2]) for g in reversed(range(q[1] // 512))]
    def b_steps(q):
        return [("B", q[0], q[1], g, q[2]) for g in range(q[1] // 512)]
    steps += a_steps(seqs_[0])
    for si, q in enumerate(seqs_):
        bs = b_steps(q)
        as_ = a_steps(seqs_[si + 1]) if si + 1 < len(seqs_) else []
        nB, nA = len(bs), len(as_)
        for j, st_ in enumerate(bs):
            steps.append(st_)
            steps += as_[(j * nA) // nB:((j + 1) * nA) // nB]
    B_PRE = [1, 2, 0, 3, 5, 7, 9, 11, 12, 8, 10, 13, 14]
    B_MLP = [15, 16, 23, 24, 17, 18, 25, 26, 19, 20, 27, 28, 21, 22, 29, 30]
    hoist_of = {}
    for i_, st_ in enumerate(steps):
        if st_[0] == "B" and i_ + 1 < len(steps) and steps[i_ + 1][0] == "A":
            hoist_of[i_] = i_ + 1
    hoisted_all = set(hoist_of.values())
    wseq = []
    for i_, st_ in enumerate(steps):
        if st_[0] == "A":
            if i_ not in hoisted_all:
                wseq += [4, 6]
        else:
            wseq += B_PRE + ([4, 6] if i_ in hoist_of else []) + B_MLP
    wstate = {"next": 0, "loaded": set()}

    def w_load(pos):
        wstate["loaded"].add(pos)
        if pos < len(wseq):
            b = wseq[pos]
            dma("sp", wring[pos % RING][:, :], wbf[b], f"w{pos % RING}")

    for p_ in range(RING):
        w_load(p_)

    def w_acquire(expect):
        pos = wstate["next"]
        wstate["next"] = pos + 1
        assert wseq[pos] == expect, (pos, wseq[pos], expect)
        assert pos in wstate["loaded"], pos
        return pos, wring[pos % RING]

    def w_release(pos):
        w_load(pos + RING)

    def rstd_from_ss(ssv, n, inv_n):
        c = stat_cols(8)
        assert n <= 4
        act(c.sub((slice(None), slice(0, n))), ssv, AF.Ln, bias=EPS, scale=inv_n)
        act(c.sub((slice(None), slice(4, 4 + n))), c.sub((slice(None), slice(0, n))), AF.Exp, scale=-0.5)
        return c.sub((slice(None), slice(4, 4 + n)))

    def prep(tok0, ubuf, uslot, gcol):
        for i in range(4):
            j = 0
            r0 = tok0 + i * 128
            dma("sp", xst[:, j, :], x_all[r0:r0 + 128, :], f"xs{j}")
            ssc = stat_cols(4)
            ss = ssc.sub((slice(None), slice(0, 1)))
            act(hn[:, j, :], xst[:, j, :], AF.Square, accum=ss)
            rs = rstd_from_ss(ss, 1, 1.0 / D)
            tsm("dve", hn[:, j, :], xst[:, j, :], rs)
            for k in range(8):
                tr(pT[:, k, :], hn[:, j, k * 128:(k + 1) * 128])
            tt("dve", ubuf[:, uslot, :, i * 128:(i + 1) * 128], pT[:, :, :],
               gfm[:, gcol:gcol + 8].bc(2, [128, 8, 128]), ALU.mult)

    def proj_tm(blk, ubuf, uslot, i, ncols, brow_row, boff, c0=0):
        p = gps()
        o = p[:, 0:ncols]
        for k in range(8):
            mm(o, ubuf[:, uslot, k, i * 128:(i + 1) * 128], blk[:, k * 512 + c0:k * 512 + c0 + ncols], k == 0, False)
        mm(o, c16[brow_row:brow_row + 1, 384:512], brow[brow_row:brow_row + 1, boff:boff + ncols], False, True)
        return o

    def proj_fm(blk, uslot, c0, M, ubuf=None):
        ubuf = uT if ubuf is None else ubuf
        p = gps()
        o = p[0:M, 0:512]
        for k in range(8):
            mm(o, blk[:, k * 512 + c0:k * 512 + c0 + M], ubuf[:, uslot, k, :], k == 0, k == 7)
        return o

    def gla_common_A(i, c, nt, par, g, b4, b6):
        tc = slice(i * 128, (i + 1) * 128)
        j = nxt("va", 2)
        o = proj_tm(b4, uTA, 0, i, 512, 0, 512)
        cp("dve", vA[:, j, :], o)
        pfree(o)
        o = proj_tm(b6, uTA, 0, i, 256, 32, 512)
        cp("act", kA[:, j, :], o)
        pfree(o)
        dma("sp", d_vtm[par, g, :, i * 512:(i + 1) * 512], vA[:, j, :], f"sv{j}")
        dma("sp", d_ktm[par, g, :, i * 256:(i + 1) * 256], kA[:, j, :], f"sk{j}")
        pz = gps()
        mm(pz[:, 0:256], lrTA[0:33, tc], gkaug[0:33, 256:512], True, True)
        act(e_t[:, 0:256], pz[:, 0:256], AF.Exp, scale=-1.0)
        pfree(pz)
        spv = spA[:, :]
        act(spv, e_t[:, 0:256], AF.Ln, bias=1.0)
        pgm = gps()
        mm(pgm[:, 0:256], SUTs, spv, True, True)
        act(GexpA[:, :], pgm[:, 0:256], AF.Exp)
        pfree(pgm)
        tt("dve", kddA[:, :], kA[:, j, :], GexpA[:, :], ALU.mult)
        pu = gps()
        for h in range(4):
            mm(pu[0:64, h * 128:(h + 1) * 128], kddA[:, h * 64:(h + 1) * 64], vA[:, j, h * 128:(h + 1) * 128], True, True)
        if c == nt - 1:
            cp("dve", R[:, :], pu[0:64, :])
        else:
            ptot = gps()
            for h in range(4):
                mm(ptot[0:64, h:h + 1], spv.sub((slice(None), slice(h * 64, (h + 1) * 64))), LTs.sub((slice(None), slice(0, 1))), True, True)
            dcol = (nxt("rb", 2)) * 4
            act(Db[:, dcol:dcol + 4], ptot[0:64, 0:4], AF.Exp)
            pfree(ptot)
            for h in range(4):
                stt("dve", R[:, h * 128:(h + 1) * 128], R[:, h * 128:(h + 1) * 128], Db[:, dcol + h:dcol + h + 1],
                    pu[0:64, h * 128:(h + 1) * 128], ALU.mult, ALU.add)
        pfree(pu)
        if c >= 1:
            cp("dve", Rbf[:, 0, :], R[:, :])
            dma("sp", rst[par, c - 1], Rbf[:, 0, :], "rs0")

    def gla_tile_B(i, c, nt, par):
        tc = slice(i * 128, (i + 1) * 128)
        if c < nt - 1:
            rj = 0
            dma("sp", Rin[:, rj, :], rst[par, c], f"ri{rj}")
        pz = gps("g")
        mm(pz[:, 0:512], lrT[0:33, tc], gkaug[0:33, 0:512], True, True)
        act(e_t[:, :], pz[:, 0:512], AF.Exp, scale=-1.0)
        pfree(pz)
        spv = sp_t[:, 0, :]
        act(spv, e_t[:, :], AF.Ln, bias=1.0)
        r4 = lambda b: b[:, :].re("p (h t) -> p h t", h=4)
        hsel = lambda b, e_: b[:, :].re("p (c e t) -> p c e t", c=2, e=2).sub((slice(None), slice(None), e_, slice(None)))
        pc = gps("g")
        for c2 in range(2):
            mm(pc[:, c2 * 128:(c2 + 1) * 128], spv.sub((slice(None), slice(c2 * 128, (c2 + 1) * 128))), UTs, True, True)
        for e_ in range(2):
            src_ = pc[e_ * 64:(e_ + 1) * 64, 0:256].re("p (c t) -> p c t", c=2)
            act(hsel(E1f, e_), src_, AF.Exp)
            act(hsel(E2f, e_), src_, AF.Exp, scale=-1.0)
        pfree(pc)
        tt("dve", qdf[:, :, :], gqT[:, :, tc], r4(E1f), ALU.mult)
        tt("dve", kdf[:, :, :], gkT[:, :, tc], r4(E2f), ALU.mult)
        cp("dve", Dfw[:, :], r4(E1f).sub((slice(None), slice(None), 127)))
        pcb = gps("g")
        for c2 in range(2):
            mm(pcb[:, c2 * 128:(c2 + 1) * 128], spv.sub((slice(None), slice(256 + c2 * 128, 256 + (c2 + 1) * 128))), LTs, True, True)
        for e_ in range(2):
            src_ = pcb[e_ * 64:(e_ + 1) * 64, 0:256].re("p (c t) -> p c t", c=2)
            act(hsel(E1b, e_), src_, AF.Exp)
            act(hsel(E2b, e_), src_, AF.Exp, scale=-1.0)
        pfree(pcb)
        tt("dve", qdb[:, :, :], gqT[:, :, tc], r4(E1b), ALU.mult)
        tt("dve", kdb[:, :, :], gkT[:, :, tc], r4(E2b), ALU.mult)
        pa = gps("g")
        for h in range(4):
            mm(pa[:, h * 128:(h + 1) * 128], kdf[:, h, :], qdf[:, h, :], True, True)
        tt("dve", Amf[:, :, :], pa[:, :].re("p (h t) -> p h t", h=4), Mf.bc(1, [128, 4, 128]), ALU.mult)
        pfree(pa)
        pab = gps("g")
        for h in range(4):
            mm(pab[:, h * 128:(h + 1) * 128], kdb[:, h, :], qdb[:, h, :], True, True)
        tt("dve", Amb[:, :, :], pab[:, :].re("p (h t) -> p h t", h=4), Mb.bc(1, [128, 4, 128]), ALU.mult)
        pfree(pab)
        po = gps("g")
        for h in range(4):
            hs = slice(h * 128, (h + 1) * 128)
            seq_ = [(Amf[:, h, :], v_tm[:, i, hs]), (Amb[:, h, :], v_tm[:, i, hs])]
            if c > 0:
                seq_.append((qdf[:, h, :], Sbf[:, hs]))
            if c < nt - 1:
                seq_.append((qdb[:, h, :], Rin[:, 0, hs]))
            for n_, (l_, r_) in enumerate(seq_):
                mm(po[:, hs], l_, r_, n_ == 0, n_ == len(seq_) - 1)
        if c < nt - 1:
            pgm = gps("g")
            mm(pgm[:, 0:256], SLTs, spv.sub((slice(None), slice(0, 256))), True, True)
            act(Gexp[:, :], pgm[:, 0:256], AF.Exp)
            pfree(pgm)
            tt("dve", kdd[:, :], k_tm[:, i, :], Gexp[:, :], ALU.mult)
            pu = gps("g")
            for h in range(4):
                mm(pu[0:64, h * 128:(h + 1) * 128], kdd[:, h * 64:(h + 1) * 64], v_tm[:, i, h * 128:(h + 1) * 128], True, True)
            if c == 0:
                cp("dve", S[:, :], pu[0:64, :])
            else:
                for h in range(4):
                    stt("dve", S[:, h * 128:(h + 1) * 128], S[:, h * 128:(h + 1) * 128],
                        Dfw[:, h:h + 1], pu[0:64, h * 128:(h + 1) * 128], ALU.mult, ALU.add)
            pfree(pu)
            cp("dve", Sbf[:, :], S[:, :])
        sq = tmpf[nxt("tmpf", 3)]
        act(sq[:, :], po[:, :], AF.Square)
        ssc = stat_cols(4)
        red(ssc, sq[:, :].re("p (h v) -> p h v", h=4))
        rs = rstd_from_ss(ssc, 4, 1.0 / 128)
        for h in range(4):
            hs = slice(h * 128, (h + 1) * 128)
            stt("dve", go[:, hs], po[:, hs], rs.sub((slice(None), slice(h, h + 1))), sg[:, i, hs], ALU.mult, ALU.mult)
        pfree(po)
        for k in range(4):
            tr(pT[:, k, :], go[:, k * 128:(k + 1) * 128])
        cp("act", glaoT[:, :, tc], pT[:, 0:4, :])

    def na_tile(i, m, info):
        tc = slice(i * 128, (i + 1) * 128)
        kts, runs = info[m]
        n = len(kts)
        nmain = min(n, 4)
        for hh in range(2):
            po = gps("n")
            for pp in range(2):
                he, ho = hh * 4 + pp * 2, hh * 4 + pp * 2 + 1
                p5 = {he: gps("n"), ho: gps("n")} if n == 5 else None

                def smm(h, idx):
                    prr, pb = h // 2, (h % 2) * 64
                    kt = kts[idx]
                    if idx < 4:
                        o_ = pS[h % 2][:, idx * 128:(idx + 1) * 128]
                    else:
                        o_ = p5[h][:, 0:128]
                    mm(o_, kT[pb:pb + 64, kt % KVS, prr, :], qT[pb:pb + 64, prr, tc], True, True)

                if NA_INTERLEAVE:
                    if n == 5:
                        smm(he, 4)
                    for idx in range(nmain):
                        smm(ho, idx)
                        smm(he, idx)
                    if n == 5:
                        smm(ho, 4)
                else:
                    for h in (he, ho):
                        for idx in range(n):
                            smm(h, idx)
                for h in (he, ho):
                    act(pexp[:, h % 2, 0:nmain * 128], pS[h % 2][:, 0:nmain * 128], AF.Exp, scale=0.125)
                if n == 5:
                    for h in (he, ho):
                        act(pexp[:, h % 2, 512:640], p5[h][:, 0:128], AF.Exp, scale=0.125)
                        pfree(p5[h])
                for h in (he, ho):
                    j = h % 2
                    for (a0, ln, p0) in runs:
                        tt("dve", PTb[:, j, a0:a0 + ln, :], pexp[:, j, a0 * 128:(a0 + ln) * 128].re("p (c q) -> p c q", c=ln),
                           EB[:, p0:p0 + ln, h, :], ALU.mult)
                for h in (he, ho):
                    j = h % 2
                    hl = h % 4
                    for idx, kt in enumerate(kts):
                        mm(po[:, hl * 65:(hl + 1) * 65], PTb[:, j, idx, :], vr[:, kt % KVS, h, :], idx == 0, idx == n - 1)
            ssc = stat_cols(4)
            pov = po[:, 0:260].re("p (h e) -> p h e", h=4)
            recip(ssc, pov.sub((slice(None), slice(None), 64)))
            tt("dve", nao[:, hh * 256:(hh + 1) * 256].re("p (h e) -> p h e", h=4),
               pov.sub((slice(None), slice(None), slice(0, 64))), ssc.bc(2, [128, 4, 64]), ALU.mult)
            pfree(po)
        for k in range(4):
            tr(pT[:, 4 + k, :], nao[:, k * 128:(k + 1) * 128])
        cp("act", naoT[:, :, tc], pT[:, 4:8, :])

    prepped = {}

    hoisted = set()
    for si_, (mode, seq_t0, T, g, par) in enumerate(steps):
        nt = T // 128
        ng = T // 512
        tok0 = seq_t0 + g * 512
        def a_prep(a_t0, a_T, a_g, a_par):
            pool_mode["A"] = True
            prep(a_t0 + a_g * 512, uTA, 0, 0)
            dma("sp", d_uT[a_par, a_g], uTA[:, 0, :, :].re("p k t -> p (k t)"), "su")
            pool_mode["A"] = False

        def a_rest(a_t0, a_T, a_g, a_par):
            pool_mode["A"] = True
            p4, b4 = w_acquire(4)
            p6, b6 = w_acquire(6)
            o = proj_fm(b6, 0, 256, 32, ubuf=uTA)
            act(lrTA[0:32, :], o, AF.Identity, bias=bfm[0:32, 16:17])
            pfree(o)
            dma("sp", d_lrT[a_par, a_g], lrTA[0:32, :], "sl")
            for i in reversed(range(4)):
                gla_common_A(i, a_g * 4 + i, a_T // 128, a_par, a_g, b4, b6)
            w_release(p4)
            w_release(p6)
            pool_mode["A"] = False

        if mode == "A":
            if si_ in hoisted_all:
                continue
            a_prep(seq_t0, T, g, par)
            a_rest(seq_t0, T, g, par)
            continue
        if si_ in hoist_of:
            a_prep(*steps[hoist_of[si_]][1:])
        info = _na_tile_info(T // 64)
        kvg = [0, 1] if g == 0 else ([g + 1] if g + 1 < ng else [])
        kvg = [x for x in kvg if x < ng]
        for gg in sorted(set([g] + kvg)):
            key = (seq_t0, gg, "B")
            if key not in prepped:
                prepped[key] = gg % 2
                dma("sp", uT[:, gg % 2, :, :].re("p k t -> p (k t)"), d_uT[par, gg], f"lu{gg % 2}")
        us = g % 2
        dma("sp", v_tm[:, :, :].re("p i c -> p (i c)"), d_vtm[par, g], "lv")
        dma("sp", k_tm[:, :, :].re("p i c -> p (i c)"), d_ktm[par, g], "lk")
        dma("sp", lrT[0:32, :], d_lrT[par, g], "ll")
        for i in range(4):
            r0 = tok0 + i * 128
            dma("sp", hbuf[:, i, :], x_all[r0:r0 + 128, :], f"h{i}")
        p1, b1 = w_acquire(1)
        for gg in kvg:
            for ch in range(4):
                o = proj_fm(b1, gg % 2, ch * 128, 128)
                s0 = (gg * 4) % KVS
                act(kT[:, s0:s0 + 4, ch, :], o.re("p (t k) -> p t k", t=4), AF.Identity, bias=bfm[:, 4 + ch:5 + ch])
                pfree(o)
        w_release(p1)
        p2, b2 = w_acquire(2)
        for gg in kvg:
            for i in range(4):
                o = proj_tm(b2, uT, gg % 2, i, 512, 0, 0)
                cp("act", vr[:, (gg * 4 + i) % KVS, :, 0:64], o.re("p (h e) -> p h e", h=8))
                pfree(o)
        w_release(p2)
        p0, b0 = w_acquire(0)
        for ch in range(4):
            o = proj_fm(b0, us, ch * 128, 128)
            act(qT[:, ch, :], o, AF.Identity, bias=bfm[:, ch:ch + 1])
            pfree(o)
        w_release(p0)
        p3, b3 = w_acquire(3)
        for c2 in range(2):
            o = proj_fm(b3, us, c2 * 128, 128)
            for hh_ in range(2):
                h = 2 * c2 + hh_
                ts("dve", gqT[:, h, :], o.buf[hh_ * 64:(hh_ + 1) * 64, 0:512], bfm[0:64, 8 + h:9 + h], 0.125, ALU.add, ALU.mult)
            pfree(o)
        for c2 in range(2):
            o = proj_fm(b3, us, 256 + c2 * 128, 128)
            for hh_ in range(2):
                h = 2 * c2 + hh_
                act(gkT[:, h, :], o.buf[hh_ * 64:(hh_ + 1) * 64, 0:512], AF.Identity, bias=bfm[0:64, 12 + h:13 + h])
            pfree(o)
        w_release(p3)
        p5, b5 = w_acquire(5)
        for i in range(4):
            o = proj_tm(b5, uT, us, i, 512, 32, 0)
            tv = tmpf[nxt("tmpf", 3)]
            act(tv[:, :], o, AF.Tanh, scale=0.5)
            tv2 = tmpf[nxt("tmpf", 3)]
            stt("dve", tv2[:, :], tv[:, :], 1.0, o, ALU.add, ALU.mult)
            pfree(o)
            tt("pool", sg[:, i, :].re("p (h v) -> p h v", h=4), tv2[:, :].re("p (h v) -> p h v", h=4),
               gglah[:, :].bc(1, [128, 4, 128]), ALU.mult)
        w_release(p5)
        for i in range(4):
            c = g * 4 + i
            gla_tile_B(i, c, nt, par)
            na_tile(i, c, info)
        p7, b7 = w_acquire(7)
        p9, b9 = w_acquire(9)
        p11, b11 = w_acquire(11)
        p12, b12 = w_acquire(12)
        p8 = p10 = None
        for fc in range(8):
            if fc == 4:
                w_release(p7)
                w_release(p9)
                p8, b8 = w_acquire(8)
                p10, b10 = w_acquire(10)
            gna = b7 if fc < 4 else b8
            ggl = b9 if fc < 4 else b10
            cc = (fc % 4) * 128
            o3 = gps()[:, :]
            for kc in range(8):
                mm(o3, gna[:, kc * 512 + cc:kc * 512 + cc + 128], uT[:, us, kc, :], kc == 0, kc == 7)
            act(tnh[:, 0, :], o3, AF.Tanh, bias=hbfm[:, fc:fc + 1], scale=0.5)
            pfree(o3)
            o4 = gps()[:, :]
            for kc in range(8):
                mm(o4, ggl[:, kc * 512 + cc:kc * 512 + cc + 128], uT[:, us, kc, :], kc == 0, kc == 7)
            act(tnh[:, 1, :], o4, AF.Tanh, bias=hbfm[:, 8 + fc:9 + fc], scale=0.5)
            pfree(o4)
            o1 = gps()[:, :]
            for kc in range(4):
                mm(o1, b11[:, kc * 1024 + fc * 128:kc * 1024 + (fc + 1) * 128], naoT[:, kc, :], kc == 0, kc == 3)
            m1 = tmpf[nxt("tmpf", 3)]
            stt("dve", m1[:, :], tnh[:, 0, :], 1.0, o1, ALU.add, ALU.mult)
            pfree(o1)
            o2 = gps()[:, :]
            for kc in range(4):
                mm(o2, b12[:, kc * 1024 + fc * 128:kc * 1024 + (fc + 1) * 128], glaoT[:, kc, :], kc == 0, kc == 3)
            m2 = tmpf[nxt("tmpf", 3)]
            stt("dve", m2[:, :], tnh[:, 1, :], 1.0, o2, ALU.add, ALU.mult)
            pfree(o2)
            tt("pool", mergedT[:, fc, :], m1[:, :], m2[:, :], ALU.add)
        w_release(p11)
        w_release(p12)
        w_release(p8)
        w_release(p10)
        pw13, bw13 = w_acquire(13)
        pw14, bw14 = w_acquire(14)
        for i in range(4):
            for half, bw in enumerate((bw13, bw14)):
                o = gps()[:, :]
                for kc in range(8):
                    mm(o, mergedT[:, kc, i * 128:(i + 1) * 128], bw[:, kc * 512:(kc + 1) * 512], kc == 0, kc == 7)
                hv = hbuf[:, i, half * 512:(half + 1) * 512]
                stt("dve", hv, o, 0.5, hv, ALU.mult, ALU.add)
                pfree(o)
            ssc = stat_cols(4)
            ss = ssc.sub((slice(None), slice(0, 1)))
            j = 0
            act(hn[:, j, :], hbuf[:, i, :], AF.Square, accum=ss)
            rs = rstd_from_ss(ss, 1, 1.0 / D)
            tsm("dve", hn[:, j, :], hbuf[:, i, :], rs)
            for k in range(8):
                tr(pT[:, k, :], hn[:, j, k * 128:(k + 1) * 128])
            tt("dve", uT[:, us, :, i * 128:(i + 1) * 128], pT[:, :, :], gfm[:, 8:16].bc(2, [128, 8, 128]), ALU.mult)
        w_release(pw13)
        w_release(pw14)
        if si_ in hoist_of:
            a_rest(*steps[hoist_of[si_]][1:])
        for q in range(4):
            pu0, bu0 = w_acquire(15 + 2 * q)
            pu1, bu1 = w_acquire(16 + 2 * q)
            pd0, bd0 = w_acquire(23 + 2 * q)
            pd1, bd1 = w_acquire(24 + 2 * q)
            hq = 0
            for fcl in range(8):
                bw = bu0 if fcl < 4 else bu1
                cc = (fcl % 4) * 128
                o = gps()[:, :]
                for kc in range(8):
                    mm(o, bw[:, kc * 512 + cc:kc * 512 + cc + 128], uT[:, us, kc, :], kc == 0, kc == 7)
                rj = 0
                act(rl[:, rj, :], o, AF.Relu)
                pfree(o)
                tt("pool", hid[:, hq, fcl, :], rl[:, rj, :], rl[:, rj, :], ALU.mult)
                if fcl == 3:
                    w_release(pu0)
            w_release(pu1)
            for i in range(4):
                for half in range(2):
                    o = gps()[:, :]
                    for kc in range(8):
                        bw = bd0 if kc < 4 else bd1
                        mm(o, hid[:, hq, kc, i * 128:(i + 1) * 128],
                           bw[:, (kc % 4) * 1024 + half * 512:(kc % 4) * 1024 + (half + 1) * 512], kc == 0, kc == 7)
                    hv = hbuf[:, i, half * 512:(half + 1) * 512]
                    tt("dve", hv, o, hv, ALU.add)
                    pfree(o)
            w_release(pd0)
            w_release(pd1)
        for i in range(4):
            ssc = stat_cols(4)
            ss = ssc.sub((slice(None), slice(0, 1)))
            act(tnh[:, :, :].re("p a c -> p (a c)"), hbuf[:, i, :], AF.Square, accum=ss)
            rs = rstd_from_ss(ss, 1, 1.0 / D)
            stt("dve", hbuf[:, i, :], hbuf[:, i, :], rs, gfin[:, :], ALU.mult, ALU.mult)
            r0 = tok0 + i * 128
            dma("sp", y_all[r0:r0 + 128, :], hbuf[:, i, :], f"y{i}")

    assert wstate["next"] == len(wseq)
    assert not plive, plive
    pr.emit(nc, es)
    print(f"[build] ops={len(pr.ops)} simulated_time_us={pr.sim_time / 1e3:.1f} busy_us=" +
          str({k: round(v / 1e3) for k, v in pr.sim_busy.items()}), flush=True)
    es.close()
    return nc, len(pr.ops)


_CACHE = {}


def kernel(**inputs):
    xp = np.asarray(inputs["x_prompt"], np.float32)
    xs = np.asarray(inputs["x_sample"], np.float32)
    bp, tp = xp.shape[0], xp.shape[1]
    bs, ts_ = xs.shape[0], xs.shape[1]
    npc, nsc = bp // NCORES, bs // NCORES
    seq_lens = [tp] * npc + [ts_] * nsc
    consts = _host_consts(inputs)
    key = tuple(seq_lens)
    if key not in _CACHE:
        _CACHE[key] = build_program(seq_lens)
    nc, _ = _CACHE[key]
    in_maps = []
    for c in range(NCORES):
        parts = [xp[c * npc + i] for i in range(npc)] + [xs[c * nsc + i] for i in range(nsc)]
        m = dict(consts)
        m["x_all"] = np.ascontiguousarray(np.concatenate(parts, 0))
        in_maps.append(m)
    res = run_bass_kernel_spmd(nc, in_maps, core_ids=list(range(NCORES)))
    yp = np.zeros_like(xp)
    ys = np.zeros_like(xs)
    for c in range(NCORES):
        y = res.results[c]["y_all"]
        o = 0
        for i in range(npc):
            yp[c * npc + i] = y[o:o + tp]
            o += tp
        for i in range(nsc):
            ys[c * nsc + i] = y[o:o + ts_]
            o += ts_
    return (yp, ys)
```

```python
import os
import numpy as np
from contextlib import ExitStack
import concourse.bass as bass
import concourse.mybir as mybir
from concourse.bass_utils import run_bass_kernel_spmd

F32 = mybir.dt.float32
BF16 = mybir.dt.bfloat16
AF = mybir.ActivationFunctionType
ALU = mybir.AluOpType
AX = mybir.AxisListType

D = 1024
NCORES = 8
NB = 31
BLK = 4096
RING = int(os.environ.get('K_RING', '4'))
EPS = 1e-6
NA_INTERLEAVE = os.environ.get('K_IL', '0') == '1'
KVS = int(os.environ.get('K_KVS', '12'))
C_NAQ, C_NAK, C_NAV = (0, 512), (512, 1024), (1024, 1536)
C_GQ, C_GK, C_GV, C_GG = (1536, 1792), (1792, 2048), (2048, 2560), (2560, 3072)
C_LRF, C_LRB = (3072, 3088), (3088, 3104)
C_GNA, C_GGLA = (3104, 4128), (4128, 5152)
EB_ORDER = [5, 4, 0, 1, 6, 2, 3, 7, 8]
NEB = len(EB_ORDER)


class View:
    __slots__ = ("buf", "ap", "lo", "hi")

    def __init__(self, buf, ap, lo, hi):
        self.buf, self.ap, self.lo, self.hi = buf, ap, lo, hi

    def re(self, pat, **kw):
        return View(self.buf, self.ap.rearrange(pat, **kw), self.lo, self.hi)

    def bc(self, axis, shape):
        return View(self.buf, self.ap.unsqueeze(axis).to_broadcast(list(shape)), self.lo, self.hi)

    def sub(self, key):
        return View(self.buf, self.ap[key], self.lo, self.hi)


class Buf:
    def __init__(self, name, t, shape, space, parent=None, base=0):
        self.name, self.t, self.shape, self.space = name, t, tuple(shape), space
        self.parent, self.base = parent, base
        self.bank_elems = 512
        n = len(shape)
        fs = [0] * n
        acc = 1
        for i in range(n - 1, 0, -1):
            fs[i] = acc
            acc *= shape[i]
        fs[0] = acc if space == "dram" else 0
        self.fs = fs
        self.wrecs = []
        self.rrecs = []

    @property
    def root(self):
        return self.parent if self.parent is not None else self

    def __getitem__(self, key):
        if not isinstance(key, tuple):
            key = (key,)
        key = key + (slice(None),) * (len(self.shape) - len(key))
        lo = hi = self.base
        for d, k in enumerate(key):
            n = self.shape[d]
            if isinstance(k, int):
                a = b = k
            else:
                a = 0 if k.start is None else k.start
                e = n if k.stop is None else k.stop
                st = 1 if k.step is None else k.step
                b = a + ((e - a - 1) // st) * st
            lo += a * self.fs[d]
            hi += b * self.fs[d]
        hi += 1
        if self.space == "psum":
            be = self.bank_elems
            lo = (lo // be) * be
            hi = -(-hi // be) * be
        return View(self, self.t[key], lo, hi)


class Op:
    __slots__ = ("eng", "fn", "deps", "sig", "sigval", "dma", "chan", "chanval", "idx", "dur", "lat", "fin", "nin", "succ")


class Prog:
    ENGS = ("pe", "act", "dve", "pool", "sp")

    def __init__(self):
        self.ops = []
        self.chan_count = {}

    def add(self, eng, fn, reads, writes, chan=None, dur=100.0, lat=None):
        op = Op()
        op.eng, op.fn, op.sig, op.sigval = eng, fn, False, 0
        op.dma = chan is not None
        op.chan = chan
        op.idx = len(self.ops)
        op.dur = dur
        op.lat = dur if lat is None else lat
        if op.dma:
            self.chan_count[chan] = self.chan_count.get(chan, 0) + 16
            op.chanval = self.chan_count[chan]
        deps = set()
        for v in reads:
            for r in v.buf.root.wrecs:
                if r[0] < v.hi and v.lo < r[1]:
                    deps.add(r[2])
        for v in writes:
            b = v.buf.root
            for r in b.wrecs:
                if r[0] < v.hi and v.lo < r[1]:
                    deps.add(r[2])
            for r in b.rrecs:
                if r[0] < v.hi and v.lo < r[1]:
                    deps.add(r[2])
        deps.discard(op)
        op.deps = deps
        for v in reads:
            v.buf.root.rrecs.append((v.lo, v.hi, op))
        for v in writes:
            b = v.buf.root
            b.wrecs = [r for r in b.wrecs if not (r[0] >= v.lo and r[1] <= v.hi)]
            b.rrecs = [r for r in b.rrecs if not (r[0] >= v.lo and r[1] <= v.hi)]
            b.wrecs.append((v.lo, v.hi, op))
        self.ops.append(op)
        return op

    def schedule(self):
        import heapq
        ops = self.ops
        for op in ops:
            op.succ = []
            op.nin = len(op.deps)
            op.fin = 0.0
        for op in ops:
            for d in op.deps:
                d.succ.append(op)
        prio_mode = os.environ.get("K_PRIO", "0.5")
        if prio_mode != "idx":
            bl = [0.0] * len(ops)
            for op in reversed(ops):
                m_ = 0.0
                for s_ in op.succ:
                    if bl[s_.idx] > m_:
                        m_ = bl[s_.idx]
                bl[op.idx] = m_ + op.lat
            tot = bl[0] if bl else 1.0
            w_ = float(prio_mode)
            scale = len(ops) / max(tot, 1.0)
            key_of = [op.idx - w_ * (bl[op.idx] * scale - (len(ops) - op.idx)) for op in ops]
        else:
            key_of = [float(op.idx) for op in ops]
        pending = {e: [] for e in self.ENGS}
        avail = {e: [] for e in self.ENGS}
        free = {e: 0.0 for e in self.ENGS}
        ready_t = {}
        for op in ops:
            if op.nin == 0:
                heapq.heappush(pending[op.eng], (0.0, op.idx))
        order = {e: [] for e in self.ENGS}
        left = len(ops)
        XLAT = 80.0
        while left:
            best = None
            for e in self.ENGS:
                pe_, av = pending[e], avail[e]
                t = free[e]
                while pe_ and pe_[0][0] <= t:
                    i_ = heapq.heappop(pe_)[1]
                    heapq.heappush(av, (key_of[i_], i_))
                if av:
                    cand = (t, av[0][1], e, True)
                elif pe_:
                    cand = (pe_[0][0], pe_[0][1], e, False)
                else:
                    continue
                if best is None or cand[:2] < best[:2]:
                    best = cand
            start, idx, e, from_av = best
            if from_av:
                heapq.heappop(avail[e])
            else:
                heapq.heappop(pending[e])
            op = ops[idx]
            order[e].append(op)
            free[e] = start + op.dur
            op.fin = start + op.lat
            left -= 1
            for s_ in op.succ:
                s_.nin -= 1
                rt = op.fin + (0.0 if (s_.eng == op.eng and not op.dma) else XLAT)
                if ready_t.get(s_.idx, 0.0) < rt:
                    ready_t[s_.idx] = rt
                if s_.nin == 0:
                    heapq.heappush(pending[s_.eng], (ready_t.get(s_.idx, 0.0), s_.idx))
        self.sim_time = max(op.fin for op in ops)
        self.sim_busy = {e: sum(op.dur for op in order[e]) for e in self.ENGS}
        return order

    def emit(self, nc, es):
        per = self.schedule()
        for op in self.ops:
            for d in op.deps:
                if (not d.dma) and not (d.eng == "pe" and op.eng == "pe"):
                    d.sig = True
        for e in self.ENGS:
            c = 0
            for op in per[e]:
                if (not op.dma) and op.sig:
                    c += 1
                    op.sigval = c
        esem = {e: es.enter_context(nc.semaphore("s_" + e)) for e in self.ENGS}
        csem = {c: es.enter_context(nc.semaphore("c_" + c)) for c in self.chan_count}
        handles = {"pe": "tensor", "act": "scalar", "dve": "vector", "pool": "gpsimd", "sp": "sync"}
        final_waits = [(csem[c], v) for c, v in self.chan_count.items()]

        def run(e, h):
            waited = {}
            for op in per[e]:
                need = {}
                for d in op.deps:
                    if d.dma:
                        k, val = csem[d.chan], d.chanval
                    elif d.eng == "pe" and e == "pe":
                        continue
                    else:
                        k, val = esem[d.eng], d.sigval
                    if need.get(k, 0) < val:
                        need[k] = val
                todo = [(k, val) for k, val in need.items() if waited.get(k, 0) < val]
                for k, val in todo:
                    waited[k] = val
                attach = todo.pop() if (todo and e != "sp") else None
                for k, val in todo:
                    h.wait_ge(k, val)
                ins = op.fn(h)
                if attach is not None:
                    ins._wait_ge(attach[0], attach[1])
                if op.dma:
                    ins.then_inc(csem[op.chan], 16)
                elif op.sig:
                    ins.then_inc(esem[e], 1)
            if e == "sp":
                for k, val in final_waits:
                    if waited.get(k, 0) < val:
                        h.wait_ge(k, val)

        block = es.enter_context(nc.Block())
        block.tensor(lambda h: run("pe", h))
        block.scalar(lambda h: run("act", h))
        block.vector(lambda h: run("dve", h))
        block.gpsimd(lambda h: run("pool", h))
        block.sync(lambda h: run("sp", h))


def _blk(w, cols, nk, ncols):
    out = np.zeros((128, nk, ncols), np.float32)
    sub = w[:, cols].reshape(nk, 128, len(cols))
    out[:, :, : len(cols)] = sub.transpose(1, 0, 2)
    return out.reshape(128, nk * ncols)


def _na_variants():
    rows, kh = 64, 8
    rs = lambda r: int(np.clip(r - 4, 0, rows - kh))
    allv = {}
    for m in range(rows // 2):
        lo, hi = rs(2 * m), rs(2 * m + 1) + 8
        for kt in range(lo // 2, (hi - 1) // 2 + 1):
            key = []
            for krl in range(2):
                for qrl in range(2):
                    kr, qr = 2 * kt + krl, 2 * m + qrl
                    key.append(kr - qr + 7 if rs(qr) <= kr < rs(qr) + 8 else None)
            allv.setdefault(tuple(key), len(allv))
    return allv


def _na_tile_info(rows):
    kh = 8
    rs = lambda r: int(np.clip(r - 4, 0, rows - kh))
    allv = _na_variants()
    info = []
    for m in range(rows // 2):
        lo, hi = rs(2 * m), rs(2 * m + 1) + 8
        kts = list(range(lo // 2, (hi - 1) // 2 + 1))
        pat = []
        for kt in kts:
            key = []
            for krl in range(2):
                for qrl in range(2):
                    kr, qr = 2 * kt + krl, 2 * m + qrl
                    key.append(kr - qr + 7 if rs(qr) <= kr < rs(qr) + 8 else None)
            pat.append(allv[tuple(key)])
        pos = [EB_ORDER.index(v) for v in pat]
        runs = []
        a = 0
        for b in range(1, len(pos) + 1):
            if b == len(pos) or pos[b] != pos[b - 1] + 1:
                runs.append((a, b - a, pos[a]))
                a = b
        info.append((kts, runs))
    return info


def _na_bias_table(rpb):
    allv = _na_variants()
    inv = {v: k for k, v in allv.items()}
    c = np.arange(64)
    col_start = np.clip(c - 8, 0, 48)
    cmask = (c[None, :] >= col_start[:, None]) & (c[None, :] < col_start[:, None] + 16)
    dc = np.clip(c[None, :] - c[:, None], -15, 15) + 15
    tab = np.full((128, NEB, 8, 128), -30000.0, np.float32)
    for e, vid in enumerate(EB_ORDER):
        key = inv[vid]
        for krl in range(2):
            for qrl in range(2):
                dr = key[krl * 2 + qrl]
                if dr is None:
                    continue
                vals = rpb[:, dr, :][:, dc]
                vals = np.where(cmask[None], vals, np.float32(-30000.0))
                tab[krl * 64:(krl + 1) * 64, e, :, qrl * 64:(qrl + 1) * 64] = vals.transpose(2, 0, 1)
    return tab.reshape(128, NEB * 8 * 128)


def _host_consts(inp):
    w_in = np.asarray(inp["w_in"][0], np.float32)
    b_in = np.asarray(inp["b_in"][0], np.float32)
    ar = lambda r: np.arange(r[0], r[1])
    blocks = []
    blocks.append(_blk(w_in, ar(C_NAQ), 8, 512))
    blocks.append(_blk(w_in, ar(C_NAK), 8, 512))
    blocks.append(_blk(w_in, ar(C_NAV), 8, 512))
    blocks.append(_blk(w_in, np.concatenate([ar(C_GQ), ar(C_GK)]), 8, 512))
    blocks.append(_blk(w_in, ar(C_GV), 8, 512))
    blocks.append(_blk(w_in, ar(C_GG), 8, 512))
    blocks.append(_blk(w_in, np.concatenate([ar(C_GK), ar(C_LRF), ar(C_LRB)]), 8, 512))
    blocks.append(_blk(w_in, ar(C_GNA)[:512], 8, 512))
    blocks.append(_blk(w_in, ar(C_GNA)[512:], 8, 512))
    blocks.append(_blk(w_in, ar(C_GGLA)[:512], 8, 512))
    blocks.append(_blk(w_in, ar(C_GGLA)[512:], 8, 512))
    blocks.append(_blk(np.asarray(inp["w_br_na"][0], np.float32), np.arange(1024), 4, 1024))
    blocks.append(_blk(np.asarray(inp["w_br_gla"][0], np.float32), np.arange(1024), 4, 1024))
    w_out = np.asarray(inp["w_out"][0], np.float32)
    blocks.append(_blk(w_out, np.arange(0, 512), 8, 512))
    blocks.append(_blk(w_out, np.arange(512, 1024), 8, 512))
    w_up = np.asarray(inp["w_up"][0], np.float32)
    for j in range(8):
        blocks.append(_blk(w_up, np.arange(512 * j, 512 * (j + 1)), 8, 512))
    w_down = np.asarray(inp["w_down"][0], np.float32)
    for j in range(8):
        blocks.append(_blk(w_down[512 * j:512 * (j + 1)], np.arange(1024), 4, 1024))
    wsrc = np.stack(blocks, 0)
    assert wsrc.shape == (NB, 128, BLK)

    bfm = np.zeros((128, 40), np.float32)
    for ch in range(4):
        bfm[:, ch] = b_in[C_NAQ[0] + ch * 128: C_NAQ[0] + (ch + 1) * 128]
        bfm[:, 4 + ch] = b_in[C_NAK[0] + ch * 128: C_NAK[0] + (ch + 1) * 128]
    for h in range(4):
        bfm[:64, 8 + h] = b_in[C_GQ[0] + h * 64: C_GQ[0] + (h + 1) * 64]
        bfm[:64, 12 + h] = b_in[C_GK[0] + h * 64: C_GK[0] + (h + 1) * 64]
    bfm[:16, 16] = b_in[C_LRF[0]:C_LRF[1]]
    bfm[16:32, 16] = b_in[C_LRB[0]:C_LRB[1]]
    for ch in range(8):
        bfm[:, 17 + ch] = b_in[C_GNA[0] + ch * 128: C_GNA[0] + (ch + 1) * 128]
        bfm[:, 25 + ch] = b_in[C_GGLA[0] + ch * 128: C_GGLA[0] + (ch + 1) * 128]
    brow = np.zeros((33, 1024), np.float32)
    brow[0, 0:512] = b_in[C_NAV[0]:C_NAV[1]]
    brow[0, 512:1024] = b_in[C_GV[0]:C_GV[1]]
    brow[32, 0:512] = b_in[C_GG[0]:C_GG[1]]
    brow[32, 512:768] = b_in[C_GK[0]:C_GK[1]]
    gk = np.zeros((33, 512), np.float32)
    gk[0:16, 0:256] = np.asarray(inp["gk_fwd_w"][0], np.float32)
    gk[16:32, 256:512] = np.asarray(inp["gk_bwd_w"][0], np.float32)
    gk[32, 0:256] = np.asarray(inp["gk_fwd_b"][0], np.float32)
    gk[32, 256:512] = np.asarray(inp["gk_bwd_b"][0], np.float32)
    gfm = np.zeros((128, 16), np.float32)
    gfm[:, 0:8] = np.asarray(inp["norm_mix_g"][0], np.float32).reshape(8, 128).T
    gfm[:, 8:16] = np.asarray(inp["norm_mlp_g"][0], np.float32).reshape(8, 128).T
    gfin = np.broadcast_to(np.asarray(inp["norm_final_g"], np.float32)[None, :], (128, 1024)).copy()
    ggla = np.broadcast_to(np.asarray(inp["gla_norm_g"][0], np.float32)[None, :], (128, 128)).copy()
    j = np.arange(128)[:, None]
    t = np.arange(128)[None, :]
    s = np.float32(-1.0 / 16.0)
    tri = np.concatenate([(j <= t) * s, (j > t) * s, (j >= t) * s, (j < t) * s], 1).astype(np.float32)
    c16 = np.concatenate([np.eye(128), (j <= t) * 1.0, (j >= t) * 1.0, np.ones((128, 128))], 1).astype(np.float32)
    natab = _na_bias_table(np.asarray(inp["na_rpb"][0], np.float32))
    return dict(wsrc=wsrc, bfm=bfm, brow=brow, gk=gk, gfm=gfm, gfin=gfin, ggla=ggla, tri=tri, c16=c16, natab=natab)


def build_program(seq_lens):
    ntok = int(sum(seq_lens))
    nc = bass.Bass("TRN2", target_bir_lowering=False)
    es = ExitStack()
    pr = Prog()

    def dram(name, shape, dt, kind):
        return Buf(name, nc.dram_tensor(name, list(shape), dt, kind=kind).ap(), shape, "dram")

    x_all = dram("x_all", [ntok, D], F32, "ExternalInput")
    y_all = dram("y_all", [ntok, D], F32, "ExternalOutput")
    wsrc = dram("wsrc", [NB, 128, BLK], F32, "ExternalInput")
    d_bfm = dram("bfm", [128, 40], F32, "ExternalInput")
    d_brow = dram("brow", [33, 1024], F32, "ExternalInput")
    d_gk = dram("gk", [33, 512], F32, "ExternalInput")
    d_gfm = dram("gfm", [128, 16], F32, "ExternalInput")
    d_gfin = dram("gfin", [128, 1024], F32, "ExternalInput")
    d_ggla = dram("ggla", [128, 128], F32, "ExternalInput")
    d_tri = dram("tri", [128, 512], F32, "ExternalInput")
    d_c16 = dram("c16", [128, 512], F32, "ExternalInput")
    d_natab = dram("natab", [128, NEB * 1024], F32, "ExternalInput")
    wbf = dram("wbf", [NB, 128, BLK], BF16, "Internal")
    rst = dram("rst", [2, 32, 64, 512], BF16, "Internal")
    d_uT = dram("d_uT", [2, 8, 128, 4096], BF16, "Internal")
    d_vtm = dram("d_vtm", [2, 8, 128, 2048], BF16, "Internal")
    d_ktm = dram("d_ktm", [2, 8, 128, 1024], BF16, "Internal")
    d_lrT = dram("d_lrT", [2, 8, 32, 512], BF16, "Internal")

    def sb(name, shape, dt):
        return Buf(name, es.enter_context(nc.sbuf_tensor("sb_" + name, list(shape), dt)), shape, "sbuf")

    def ps(name, shape, dt):
        return Buf(name, es.enter_context(nc.psum_tensor("ps_" + name, list(shape), dt)), shape, "psum")

    wring = [sb(f"wr{i}", [128, BLK], BF16) for i in range(RING)]
    c16 = sb("c16", [128, 512], BF16)
    tri = sb("tri", [128, 512], F32)
    bfm = sb("bfm", [128, 40], F32)
    hbfm = sb("hbfm", [128, 16], F32)
    brow = sb("brow", [33, 1024], BF16)
    gkaug = sb("gkaug", [33, 512], BF16)
    gfm = sb("gfm", [128, 16], F32)
    gfin = sb("gfin", [128, 1024], F32)
    gglah = sb("gglah", [128, 128], F32)
    EB = sb("EB", [128, NEB, 8, 128], BF16)
    xst = sb("xst", [128, 1, 1024], F32)
    hn = sb("hn", [128, 1, 1024], BF16)
    stat = sb("stat", [128, 128], F32)
    uT = sb("uT", [128, 2, 8, 512], BF16)
    hbuf = sb("hbuf", [128, 4, 1024], F32)
    kT = sb("kT", [128, KVS, 4, 128], BF16)
    vr = sb("vr", [128, KVS, 8, 65], BF16)
    qT = sb("qT", [128, 4, 512], BF16)
    ar = sb("ar", [128, 4096], BF16)
    gqT = Buf("gqT", ar.t[0:64, 0:2048].rearrange("p (h t) -> p h t", h=4), [64, 4, 512], "sbuf", parent=ar, base=0)
    gkT = Buf("gkT", ar.t[0:64, 2048:4096].rearrange("p (h t) -> p h t", h=4), [64, 4, 512], "sbuf", parent=ar, base=2048)
    mergedT = Buf("mergedT", ar.t[:, :].rearrange("p (k t) -> p k t", k=8), [128, 8, 512], "sbuf", parent=ar, base=0)
    hid = Buf("hid", ar.t[:, :].rearrange("p (a k t) -> p a k t", a=1, k=8), [128, 1, 8, 512], "sbuf", parent=ar, base=0)
    v_tm = sb("v_tm", [128, 4, 512], BF16)
    sg = sb("sg", [128, 4, 512], BF16)
    k_tm = sb("k_tm", [128, 4, 256], BF16)
    lrT = sb("lrT", [33, 512], BF16)
    uTA = sb("uTA", [128, 1, 8, 512], BF16)
    vA = sb("vA", [128, 2, 512], BF16)
    kA = sb("kA", [128, 2, 256], BF16)
    lrTA = sb("lrTA", [33, 512], BF16)
    spA = sb("spA", [128, 256], F32)
    GexpA = sb("GexpA", [128, 256], F32)
    kddA = sb("kddA", [128, 256], BF16)
    tmpf = [sb(f"tmpf{i}", [128, 512], F32) for i in range(4)]
    e_t = tmpf[3]
    sp_t = sb("sp_t", [128, 1, 512], F32)
    E1f = sb("E1f", [64, 512], F32)
    E2f = sb("E2f", [64, 512], F32)
    E1b = E1f
    E2b = E2f
    Dfw = sb("Dfw", [64, 4], F32)
    Gexp = sb("Gexp", [128, 256], F32)
    Db = sb("Db", [64, 8], F32)
    qdf = sb("qdf", [64, 4, 128], BF16)
    kdf = sb("kdf", [64, 4, 128], BF16)
    qdb = sb("qdb", [64, 4, 128], BF16)
    kdb = sb("kdb", [64, 4, 128], BF16)
    kdd = sb("kdd", [128, 256], BF16)
    Amf = sb("Amf", [128, 4, 128], BF16)
    Amb = sb("Amb", [128, 4, 128], BF16)
    go = sb("go", [128, 512], BF16)
    S = sb("S", [64, 512], F32)
    Sbf = sb("Sbf", [64, 512], BF16)
    R = sb("R", [64, 512], F32)
    Rbf = sb("Rbf", [64, 1, 512], BF16)
    Rin = sb("Rin", [64, 1, 512], BF16)
    pexp = sb("pexp", [128, 2, 640], BF16)
    PTb = sb("PTb", [128, 2, 5, 128], BF16)
    nao = sb("nao", [128, 512], BF16)
    naoT = sb("naoT", [128, 4, 512], BF16)
    glaoT = sb("glaoT", [128, 4, 512], BF16)
    tnh = sb("tnh", [128, 2, 512], BF16)
    rl = sb("rl", [128, 1, 512], BF16)
    NPG = 5
    pg = [ps(f"pg{i}", [128, 512], F32) for i in range(NPG)]
    pS = [ps(f"pS{i}", [128, 512], F32) for i in range(2)]
    pT = ps("pT", [128, 8, 128], BF16)
    pT.bank_elems = 1024

    try:
        print("[build] sbuf bytes remaining per partition:", nc.sbuf_bytes_remaining // 128 if nc.sbuf_bytes_remaining > 300000 else nc.sbuf_bytes_remaining, flush=True)
    except Exception as ex:
        print("[build] sbuf remaining n/a", ex)
    cnt = {"va": 0, "pg": 0, "stat": 0, "xst": 0, "tmpf": 0, "pS": 0, "na": 0, "rb": 0, "ri": 0, "rl": 0, "tn": 0}

    def nxt(key, n):
        v = cnt[key]
        cnt[key] = v + 1
        return v % n

    pool_mode = {"A": False}

    pfree_list = list(range(NPG))
    plive = set()

    def gps(kind="d"):
        assert pfree_list, "out of PSUM banks (missing pfree?)"
        b_ = pfree_list.pop(0)
        plive.add(b_)
        return pg[b_]

    def pfree(x):
        b_ = x.buf if isinstance(x, View) else x
        i_ = pg.index(b_)
        assert i_ in plive, i_
        plive.discard(i_)
        pfree_list.append(i_)

    def stat_cols(n=4):
        c = nxt("stat", 16) * 8
        return stat[:, c:c + n]

    ident = c16[:, 0:128]
    Mf = c16[:, 128:256]
    Mb = c16[:, 256:384]
    ones_row = c16[0:1, 384:512]
    UTs, SLTs, LTs, SUTs = tri[:, 0:128], tri[:, 128:256], tri[:, 256:384], tri[:, 384:512]

    def nfree(v):
        n = 1
        for d_ in v.ap.shape[1:]:
            n *= int(d_)
        return n

    def is16(v):
        return v.ap.dtype == BF16 and v.buf.space == "sbuf"

    def vcost(eng, out, ins):
        n = nfree(out)
        if eng == "pool":
            return 150.0 + 1.7 * n
        f = 0.55 if (is16(out) and all(is16(i) for i in ins)) else 1.04
        return 150.0 + f * n

    def mm(out, lhsT, rhs, start, stop):
        n = nfree(rhs) * (4 if rhs.ap.dtype == F32 else 1)
        dur = n / 2.05 + (85.0 if n < 256 else 5.0)
        pr.add("pe", lambda e, o=out.ap, l=lhsT.ap, r=rhs.ap, s0=start, s1=stop:
               e.matmul(o, lhsT=l, rhs=r, start=s0, stop=s1), [lhsT, rhs], [out], dur=dur, lat=dur + 120.0)

    def tr(out, in_):
        pr.add("pe", lambda e, o=out.ap, i=in_.ap, d=ident.ap: e.transpose(o, i, d), [in_, ident], [out],
               dur=160.0, lat=300.0)

    def act(out, in_, func, bias=None, scale=1.0, accum=None):
        reads = [in_]
        kw = {}
        if isinstance(bias, View):
            reads.append(bias)
            kw["bias"] = bias.ap
        elif bias is not None:
            kw["bias"] = float(bias)
        if isinstance(scale, View):
            reads.append(scale)
            kw["scale"] = scale.ap
        else:
            kw["scale"] = float(scale)
        writes = [out]
        if accum is not None:
            writes.append(accum)
            kw["accum_out"] = accum.ap
        d_ = 200.0 + 0.83 * nfree(out)
        pr.add("act", lambda e, o=out.ap, i=in_.ap, f=func, kw=kw: e.activation(out=o, in_=i, func=f, **kw),
               reads, writes, dur=d_, lat=d_ + 60.0)

    def _h(e, name):
        return getattr(e, name)

    def tt(eng, out, in0, in1, op):
        d_ = vcost(eng, out, [in0, in1])
        pr.add(eng, lambda e, o=out.ap, a=in0.ap, b=in1.ap, op=op: e.tensor_tensor(o, a, b, op), [in0, in1], [out],
               dur=d_, lat=d_ + 60.0)

    def ts(eng, out, in0, s1, s2, op0, op1):
        reads = [in0]
        a1 = s1.ap if isinstance(s1, View) else float(s1)
        a2 = s2.ap if isinstance(s2, View) else float(s2)
        reads += [s for s in (s1, s2) if isinstance(s, View)]
        d_ = vcost(eng, out, [in0])
        pr.add(eng, lambda e, o=out.ap, a=in0.ap, a1=a1, a2=a2, op0=op0, op1=op1:
               e.tensor_scalar(o, a, a1, a2, op0, op1), reads, [out], dur=d_, lat=d_ + 60.0)

    def tsm(eng, out, in0, s1):
        reads = [in0] + ([s1] if isinstance(s1, View) else [])
        a1 = s1.ap if isinstance(s1, View) else float(s1)
        d_ = vcost(eng, out, [in0])
        pr.add(eng, lambda e, o=out.ap, a=in0.ap, a1=a1: e.tensor_scalar_mul(o, a, a1), reads, [out], dur=d_, lat=d_ + 60.0)

    def stt(eng, out, in0, sc, in1, op0, op1):
        reads = [in0, in1] + ([sc] if isinstance(sc, View) else [])
        a = sc.ap if isinstance(sc, View) else float(sc)
        d_ = vcost(eng, out, [in0, in1])
        pr.add(eng, lambda e, o=out.ap, i0=in0.ap, a=a, i1=in1.ap, op0=op0, op1=op1:
               e.scalar_tensor_tensor(o, i0, a, i1, op0, op1), reads, [out], dur=d_, lat=d_ + 60.0)

    def cp(eng, out, in_):
        if eng == "act":
            d_ = 200.0 + 0.83 * nfree(out)
            pr.add("act", lambda e, o=out.ap, i=in_.ap: e.copy(o, i), [in_], [out], dur=d_, lat=d_ + 60.0)
        else:
            d_ = vcost(eng, out, [in_])
            pr.add(eng, lambda e, o=out.ap, i=in_.ap: e.tensor_copy(o, i), [in_], [out], dur=d_, lat=d_ + 60.0)

    def red(out, in_, op=ALU.add):
        d_ = 150.0 + 1.04 * nfree(in_)
        pr.add("dve", lambda e, o=out.ap, i=in_.ap, op=op: e.tensor_reduce(o, i, AX.X, op), [in_], [out], dur=d_, lat=d_ + 60.0)

    def recip(out, in_):
        pr.add("dve", lambda e, o=out.ap, i=in_.ap: e.reciprocal(o, i), [in_], [out], dur=200.0, lat=260.0)

    def memset(eng, v, val):
        pr.add(eng, lambda e, o=v.ap, val=val: e.memset(o, val), [], [v], dur=150.0 + nfree(v), lat=200.0 + nfree(v))

    def dma(q, out, in_, chan):
        nb = 1
        for d_ in out.ap.shape:
            nb *= int(d_)
        nb *= 2 if out.ap.dtype == BF16 else 4
        pr.add(q, lambda e, o=out.ap, i=in_.ap: e.dma_start(out=o, in_=i), [in_], [out], chan=chan,
               dur=60.0, lat=2200.0 + nb / 150.0)

    for i in range(0, NB, 4):
        j = min(NB, i + 4)
        dma("pool", wbf[i:j], wsrc[i:j], f"wc{i}")
    dma("pool", c16[:, :], d_c16[:, :], "k0")
    dma("sp", tri[:, :], d_tri[:, :], "k1")
    dma("sp", bfm[:, :], d_bfm[:, :], "k2")
    dma("pool", brow[:, :], d_brow[:, :], "k3")
    dma("pool", gkaug[:, :], d_gk[:, :], "k4")
    dma("sp", gfm[:, :], d_gfm[:, :], "k5")
    dma("sp", gfin[:, :], d_gfin[:, :], "k6")
    dma("sp", gglah[:, :], d_ggla[:, :], "k7")
    ts("dve", gglah[:, :], gglah[:, :], 0.5, 0.0, ALU.mult, ALU.add)
    ts("dve", hbfm[:, :], bfm[:, 17:33], 0.5, 0.0, ALU.mult, ALU.add)
    memset("pool", vr[:, :, :, :], 1.0)
    memset("pool", lrT[:, :], 1.0)
    memset("pool", lrTA[:, :], 1.0)
    for e in range(NEB):
        t0 = tmpf[e % 2]
        t1 = tmpf[2]
        for hf in range(2):
            dma("sp", t0[:, :], d_natab[:, e * 1024 + hf * 512: e * 1024 + (hf + 1) * 512], f"nt{e % 2}")
            act(t1[:, :], t0[:, :], AF.Exp)
            cp("dve", EB[:, e, hf * 4:(hf + 1) * 4, :], t1[:, :].re("p (h q) -> p h q", h=4))

    def step_blocks(mode):
        if mode == "A":
            return [4, 6]
        return [1, 2, 0, 3, 5, 7, 9, 11, 12, 8, 10, 13, 14,
                15, 16, 23, 24, 17, 18, 25, 26, 19, 20, 27, 28, 21, 22, 29, 30]

    steps = []
    t0 = 0
    seqs_ = []
    for si, T in enumerate(seq_lens):
        seqs_.append((t0, T, si % 2))
        t0 += T
    def a_steps(q):
        return [("A", q[0], q[1], g, q[2]) for g in reversed(range(q[1] // 512))]
    def b_steps(q):
        return [("B", q[0], q[1], g, q[2]) for g in range(q[1] // 512)]
    steps += a_steps(seqs_[0])
    for si, q in enumerate(seqs_):
        bs = b_steps(q)
        as_ = a_steps(seqs_[si + 1]) if si + 1 < len(seqs_) else []
        nB, nA = len(bs), len(as_)
        for j, st_ in enumerate(bs):
            steps.append(st_)
            steps += as_[(j * nA) // nB:((j + 1) * nA) // nB]
    B_PRE = [1, 2, 0, 3, 5, 7, 9, 11, 12, 8, 10, 13, 14]
    B_MLP = [15, 16, 23, 24, 17, 18, 25, 26, 19, 20, 27, 28, 21, 22, 29, 30]
    hoist_of = {}
    for i_, st_ in enumerate(steps):
        if st_[0] == "B" and i_ + 1 < len(steps) and steps[i_ + 1][0] == "A":
            hoist_of[i_] = i_ + 1
    hoisted_all = set(hoist_of.values())
    wseq = []
    for i_, st_ in enumerate(steps):
        if st_[0] == "A":
            if i_ not in hoisted_all:
                wseq += [4, 6]
        else:
            wseq += B_PRE + ([4, 6] if i_ in hoist_of else []) + B_MLP
    wstate = {"next": 0, "loaded": set()}

    def w_load(pos):
        wstate["loaded"].add(pos)
        if pos < len(wseq):
            b = wseq[pos]
            dma("sp", wring[pos % RING][:, :], wbf[b], f"w{pos % RING}")

    for p_ in range(RING):
        w_load(p_)

    def w_acquire(expect):
        pos = wstate["next"]
        wstate["next"] = pos + 1
        assert wseq[pos] == expect, (pos, wseq[pos], expect)
        assert pos in wstate["loaded"], pos
        return pos, wring[pos % RING]

    def w_release(pos):
        w_load(pos + RING)

    def rstd_from_ss(ssv, n, inv_n):
        c = stat_cols(8)
        assert n <= 4
        act(c.sub((slice(None), slice(0, n))), ssv, AF.Ln, bias=EPS, scale=inv_n)
        act(c.sub((slice(None), slice(4, 4 + n))), c.sub((slice(None), slice(0, n))), AF.Exp, scale=-0.5)
        return c.sub((slice(None), slice(4, 4 + n)))

    def prep(tok0, ubuf, uslot, gcol):
        for i in range(4):
            j = 0
            r0 = tok0 + i * 128
            dma("sp", xst[:, j, :], x_all[r0:r0 + 128, :], f"xs{j}")
            ssc = stat_cols(4)
            ss = ssc.sub((slice(None), slice(0, 1)))
            act(hn[:, j, :], xst[:, j, :], AF.Square, accum=ss)
            rs = rstd_from_ss(ss, 1, 1.0 / D)
            tsm("dve", hn[:, j, :], xst[:, j, :], rs)
            for k in range(8):
                tr(pT[:, k, :], hn[:, j, k * 128:(k + 1) * 128])
            tt("dve", ubuf[:, uslot, :, i * 128:(i + 1) * 128], pT[:, :, :],
               gfm[:, gcol:gcol + 8].bc(2, [128, 8, 128]), ALU.mult)

    def proj_tm(blk, ubuf, uslot, i, ncols, brow_row, boff, c0=0):
        p = gps()
        o = p[:, 0:ncols]
        for k in range(8):
            mm(o, ubuf[:, uslot, k, i * 128:(i + 1) * 128], blk[:, k * 512 + c0:k * 512 + c0 + ncols], k == 0, False)
        mm(o, c16[brow_row:brow_row + 1, 384:512], brow[brow_row:brow_row + 1, boff:boff + ncols], False, True)
        return o

    def proj_fm(blk, uslot, c0, M, ubuf=None):
        ubuf = uT if ubuf is None else ubuf
        p = gps()
        o = p[0:M, 0:512]
        for k in range(8):
            mm(o, blk[:, k * 512 + c0:k * 512 + c0 + M], ubuf[:, uslot, k, :], k == 0, k == 7)
        return o

    def gla_common_A(i, c, nt, par, g, b4, b6):
        tc = slice(i * 128, (i + 1) * 128)
        j = nxt("va", 2)
        o = proj_tm(b4, uTA, 0, i, 512, 0, 512)
        cp("dve", vA[:, j, :], o)
        pfree(o)
        o = proj_tm(b6, uTA, 0, i, 256, 32, 512)
        cp("act", kA[:, j, :], o)
        pfree(o)
        dma("sp", d_vtm[par, g, :, i * 512:(i + 1) * 512], vA[:, j, :], f"sv{j}")
        dma("sp", d_ktm[par, g, :, i * 256:(i + 1) * 256], kA[:, j, :], f"sk{j}")
        pz = gps()
        mm(pz[:, 0:256], lrTA[0:33, tc], gkaug[0:33, 256:512], True, True)
        act(e_t[:, 0:256], pz[:, 0:256], AF.Exp, scale=-1.0)
        pfree(pz)
        spv = spA[:, :]
        act(spv, e_t[:, 0:256], AF.Ln, bias=1.0)
        pgm = gps()
        mm(pgm[:, 0:256], SUTs, spv, True, True)
        act(GexpA[:, :], pgm[:, 0:256], AF.Exp)
        pfree(pgm)
        tt("dve", kddA[:, :], kA[:, j, :], GexpA[:, :], ALU.mult)
        pu = gps()
        for h in range(4):
            mm(pu[0:64, h * 128:(h + 1) * 128], kddA[:, h * 64:(h + 1) * 64], vA[:, j, h * 128:(h + 1) * 128], True, True)
        if c == nt - 1:
            cp("dve", R[:, :], pu[0:64, :])
        else:
            ptot = gps()
            for h in range(4):
                mm(ptot[0:64, h:h + 1], spv.sub((slice(None), slice(h * 64, (h + 1) * 64))), LTs.sub((slice(None), slice(0, 1))), True, True)
            dcol = (nxt("rb", 2)) * 4
            act(Db[:, dcol:dcol + 4], ptot[0:64, 0:4], AF.Exp)
            pfree(ptot)
            for h in range(4):
                stt("dve", R[:, h * 128:(h + 1) * 128], R[:, h * 128:(h + 1) * 128], Db[:, dcol + h:dcol + h + 1],
                    pu[0:64, h * 128:(h + 1) * 128], ALU.mult, ALU.add)
        pfree(pu)
        if c >= 1:
            cp("dve", Rbf[:, 0, :], R[:, :])
            dma("sp", rst[par, c - 1], Rbf[:, 0, :], "rs0")

    def gla_tile_B(i, c, nt, par):
        tc = slice(i * 128, (i + 1) * 128)
        if c < nt - 1:
            rj = 0
            dma("sp", Rin[:, rj, :], rst[par, c], f"ri{rj}")
        pz = gps("g")
        mm(pz[:, 0:512], lrT[0:33, tc], gkaug[0:33, 0:512], True, True)
        act(e_t[:, :], pz[:, 0:512], AF.Exp, scale=-1.0)
        pfree(pz)
        spv = sp_t[:, 0, :]
        act(spv, e_t[:, :], AF.Ln, bias=1.0)
        pc = gps("g")
        for h in range(4):
            mm(pc[0:64, h * 128:(h + 1) * 128], spv.sub((slice(None), slice(h * 64, (h + 1) * 64))), UTs, True, True)
        r4 = lambda b: b[:, :].re("p (h t) -> p h t", h=4)
        act(E1f[:, :], pc[0:64, :], AF.Exp)
        act(E2f[:, :], pc[0:64, :], AF.Exp, scale=-1.0)
        pfree(pc)
        tt("dve", qdf[:, :, :], gqT[:, :, tc], r4(E1f), ALU.mult)
        tt("dve", kdf[:, :, :], gkT[:, :, tc], r4(E2f), ALU.mult)
        cp("dve", Dfw[:, :], r4(E1f).sub((slice(None), slice(None), 127)))
        pcb = gps("g")
        for h in range(4):
            mm(pcb[0:64, h * 128:(h + 1) * 128], spv.sub((slice(None), slice(256 + h * 64, 256 + (h + 1) * 64))), LTs, True, True)
        act(E1b[:, :], pcb[0:64, :], AF.Exp)
        act(E2b[:, :], pcb[0:64, :], AF.Exp, scale=-1.0)
        pfree(pcb)
        tt("dve", qdb[:, :, :], gqT[:, :, tc], r4(E1b), ALU.mult)
        tt("dve", kdb[:, :, :], gkT[:, :, tc], r4(E2b), ALU.mult)
        pa = gps("g")
        for h in range(4):
            mm(pa[:, h * 128:(h + 1) * 128], kdf[:, h, :], qdf[:, h, :], True, True)
        tt("dve", Amf[:, :, :], pa[:, :].re("p (h t) -> p h t", h=4), Mf.bc(1, [128, 4, 128]), ALU.mult)
        pfree(pa)
        pab = gps("g")
        for h in range(4):
            mm(pab[:, h * 128:(h + 1) * 128], kdb[:, h, :], qdb[:, h, :], True, True)
        tt("dve", Amb[:, :, :], pab[:, :].re("p (h t) -> p h t", h=4), Mb.bc(1, [128, 4, 128]), ALU.mult)
        pfree(pab)
        po = gps("g")
        for h in range(4):
            hs = slice(h * 128, (h + 1) * 128)
            seq_ = [(Amf[:, h, :], v_tm[:, i, hs]), (Amb[:, h, :], v_tm[:, i, hs])]
            if c > 0:
                seq_.append((qdf[:, h, :], Sbf[:, hs]))
            if c < nt - 1:
                seq_.append((qdb[:, h, :], Rin[:, 0, hs]))
            for n_, (l_, r_) in enumerate(seq_):
                mm(po[:, hs], l_, r_, n_ == 0, n_ == len(seq_) - 1)
        if c < nt - 1:
            pgm = gps("g")
            mm(pgm[:, 0:256], SLTs, spv.sub((slice(None), slice(0, 256))), True, True)
            act(Gexp[:, :], pgm[:, 0:256], AF.Exp)
            pfree(pgm)
            tt("dve", kdd[:, :], k_tm[:, i, :], Gexp[:, :], ALU.mult)
            pu = gps("g")
            for h in range(4):
                mm(pu[0:64, h * 128:(h + 1) * 128], kdd[:, h * 64:(h + 1) * 64], v_tm[:, i, h * 128:(h + 1) * 128], True, True)
            if c == 0:
                cp("dve", S[:, :], pu[0:64, :])
            else:
                for h in range(4):
                    stt("dve", S[:, h * 128:(h + 1) * 128], S[:, h * 128:(h + 1) * 128],
                        Dfw[:, h:h + 1], pu[0:64, h * 128:(h + 1) * 128], ALU.mult, ALU.add)
            pfree(pu)
            cp("dve", Sbf[:, :], S[:, :])
        sq = tmpf[nxt("tmpf", 3)]
        act(sq[:, :], po[:, :], AF.Square)
        ssc = stat_cols(4)
        red(ssc, sq[:, :].re("p (h v) -> p h v", h=4))
        rs = rstd_from_ss(ssc, 4, 1.0 / 128)
        for h in range(4):
            hs = slice(h * 128, (h + 1) * 128)
            stt("dve", go[:, hs], po[:, hs], rs.sub((slice(None), slice(h, h + 1))), sg[:, i, hs], ALU.mult, ALU.mult)
        pfree(po)
        for k in range(4):
            tr(pT[:, k, :], go[:, k * 128:(k + 1) * 128])
        cp("act", glaoT[:, :, tc], pT[:, 0:4, :])

    def na_tile(i, m, info):
        tc = slice(i * 128, (i + 1) * 128)
        kts, runs = info[m]
        n = len(kts)
        nmain = min(n, 4)
        for hh in range(2):
            po = gps("n")
            for pp in range(2):
                he, ho = hh * 4 + pp * 2, hh * 4 + pp * 2 + 1
                p5 = {he: gps("n"), ho: gps("n")} if n == 5 else None

                def smm(h, idx):
                    prr, pb = h // 2, (h % 2) * 64
                    kt = kts[idx]
                    if idx < 4:
                        o_ = pS[h % 2][:, idx * 128:(idx + 1) * 128]
                    else:
                        o_ = p5[h][:, 0:128]
                    mm(o_, kT[pb:pb + 64, kt % KVS, prr, :], qT[pb:pb + 64, prr, tc], True, True)

                if NA_INTERLEAVE:
                    if n == 5:
                        smm(he, 4)
                    for idx in range(nmain):
                        smm(ho, idx)
                        smm(he, idx)
                    if n == 5:
                        smm(ho, 4)
                else:
                    for h in (he, ho):
                        for idx in range(n):
                            smm(h, idx)
                for h in (he, ho):
                    act(pexp[:, h % 2, 0:nmain * 128], pS[h % 2][:, 0:nmain * 128], AF.Exp, scale=0.125)
                if n == 5:
                    for h in (he, ho):
                        act(pexp[:, h % 2, 512:640], p5[h][:, 0:128], AF.Exp, scale=0.125)
                        pfree(p5[h])
                for h in (he, ho):
                    j = h % 2
                    for (a0, ln, p0) in runs:
                        tt("dve", PTb[:, j, a0:a0 + ln, :], pexp[:, j, a0 * 128:(a0 + ln) * 128].re("p (c q) -> p c q", c=ln),
                           EB[:, p0:p0 + ln, h, :], ALU.mult)
                for h in (he, ho):
                    j = h % 2
                    hl = h % 4
                    for idx, kt in enumerate(kts):
                        mm(po[:, hl * 65:(hl + 1) * 65], PTb[:, j, idx, :], vr[:, kt % KVS, h, :], idx == 0, idx == n - 1)
            ssc = stat_cols(4)
            pov = po[:, 0:260].re("p (h e) -> p h e", h=4)
            recip(ssc, pov.sub((slice(None), slice(None), 64)))
            tt("dve", nao[:, hh * 256:(hh + 1) * 256].re("p (h e) -> p h e", h=4),
               pov.sub((slice(None), slice(None), slice(0, 64))), ssc.bc(2, [128, 4, 64]), ALU.mult)
            pfree(po)
        for k in range(4):
            tr(pT[:, 4 + k, :], nao[:, k * 128:(k + 1) * 128])
        cp("act", naoT[:, :, tc], pT[:, 4:8, :])

    prepped = {}

    hoisted = set()
    for si_, (mode, seq_t0, T, g, par) in enumerate(steps):
        nt = T // 128
        ng = T // 512
        tok0 = seq_t0 + g * 512
        def a_prep(a_t0, a_T, a_g, a_par):
            pool_mode["A"] = True
            prep(a_t0 + a_g * 512, uTA, 0, 0)
            dma("sp", d_uT[a_par, a_g], uTA[:, 0, :, :].re("p k t -> p (k t)"), "su")
            pool_mode["A"] = False

        def a_rest(a_t0, a_T, a_g, a_par):
            pool_mode["A"] = True
            p4, b4 = w_acquire(4)
            p6, b6 = w_acquire(6)
            o = proj_fm(b6, 0, 256, 32, ubuf=uTA)
            act(lrTA[0:32, :], o, AF.Identity, bias=bfm[0:32, 16:17])
            pfree(o)
            dma("sp", d_lrT[a_par, a_g], lrTA[0:32, :], "sl")
            for i in reversed(range(4)):
                gla_common_A(i, a_g * 4 + i, a_T // 128, a_par, a_g, b4, b6)
            w_release(p4)
            w_release(p6)
            pool_mode["A"] = False

        if mode == "A":
            if si_ in hoisted_all:
                continue
            a_prep(seq_t0, T, g, par)
            a_rest(seq_t0, T, g, par)
            continue
        if si_ in hoist_of:
            a_prep(*steps[hoist_of[si_]][1:])
        info = _na_tile_info(T // 64)
        kvg = [0, 1] if g == 0 else ([g + 1] if g + 1 < ng else [])
        kvg = [x for x in kvg if x < ng]
        for gg in sorted(set([g] + kvg)):
            key = (seq_t0, gg, "B")
            if key not in prepped:
                prepped[key] = gg % 2
                dma("sp", uT[:, gg % 2, :, :].re("p k t -> p (k t)"), d_uT[par, gg], f"lu{gg % 2}")
        us = g % 2
        dma("sp", v_tm[:, :, :].re("p i c -> p (i c)"), d_vtm[par, g], "lv")
        dma("sp", k_tm[:, :, :].re("p i c -> p (i c)"), d_ktm[par, g], "lk")
        dma("sp", lrT[0:32, :], d_lrT[par, g], "ll")
        for i in range(4):
            r0 = tok0 + i * 128
            dma("sp", hbuf[:, i, :], x_all[r0:r0 + 128, :], f"h{i}")
        p1, b1 = w_acquire(1)
        for gg in kvg:
            for ch in range(4):
                o = proj_fm(b1, gg % 2, ch * 128, 128)
                s0 = (gg * 4) % KVS
                act(kT[:, s0:s0 + 4, ch, :], o.re("p (t k) -> p t k", t=4), AF.Identity, bias=bfm[:, 4 + ch:5 + ch])
                pfree(o)
        w_release(p1)
        p2, b2 = w_acquire(2)
        for gg in kvg:
            for i in range(4):
                o = proj_tm(b2, uT, gg % 2, i, 512, 0, 0)
                cp("act", vr[:, (gg * 4 + i) % KVS, :, 0:64], o.re("p (h e) -> p h e", h=8))
                pfree(o)
        w_release(p2)
        p0, b0 = w_acquire(0)
        for ch in range(4):
            o = proj_fm(b0, us, ch * 128, 128)
            act(qT[:, ch, :], o, AF.Identity, bias=bfm[:, ch:ch + 1])
            pfree(o)
        w_release(p0)
        p3, b3 = w_acquire(3)
        for c2 in range(2):
            o = proj_fm(b3, us, c2 * 128, 128)
            for hh_ in range(2):
                h = 2 * c2 + hh_
                ts("dve", gqT[:, h, :], o.buf[hh_ * 64:(hh_ + 1) * 64, 0:512], bfm[0:64, 8 + h:9 + h], 0.125, ALU.add, ALU.mult)
            pfree(o)
        for c2 in range(2):
            o = proj_fm(b3, us, 256 + c2 * 128, 128)
            for hh_ in range(2):
                h = 2 * c2 + hh_
                act(gkT[:, h, :], o.buf[hh_ * 64:(hh_ + 1) * 64, 0:512], AF.Identity, bias=bfm[0:64, 12 + h:13 + h])
            pfree(o)
        w_release(p3)
        p5, b5 = w_acquire(5)
        for i in range(4):
            o = proj_tm(b5, uT, us, i, 512, 32, 0)
            tv = tmpf[nxt("tmpf", 3)]
            act(tv[:, :], o, AF.Tanh, scale=0.5)
            tv2 = tmpf[nxt("tmpf", 3)]
            stt("dve", tv2[:, :], tv[:, :], 1.0, o, ALU.add, ALU.mult)
            pfree(o)
            tt("pool", sg[:, i, :].re("p (h v) -> p h v", h=4), tv2[:, :].re("p (h v) -> p h v", h=4),
               gglah[:, :].bc(1, [128, 4, 128]), ALU.mult)
        w_release(p5)
        for i in range(4):
            c = g * 4 + i
            gla_tile_B(i, c, nt, par)
            na_tile(i, c, info)
        p7, b7 = w_acquire(7)
        p9, b9 = w_acquire(9)
        p11, b11 = w_acquire(11)
        p12, b12 = w_acquire(12)
        p8 = p10 = None
        for fc in range(8):
            if fc == 4:
                w_release(p7)
                w_release(p9)
                p8, b8 = w_acquire(8)
                p10, b10 = w_acquire(10)
            gna = b7 if fc < 4 else b8
            ggl = b9 if fc < 4 else b10
            cc = (fc % 4) * 128
            o3 = gps()[:, :]
            for kc in range(8):
                mm(o3, gna[:, kc * 512 + cc:kc * 512 + cc + 128], uT[:, us, kc, :], kc == 0, kc == 7)
            act(tnh[:, 0, :], o3, AF.Tanh, bias=hbfm[:, fc:fc + 1], scale=0.5)
            pfree(o3)
            o4 = gps()[:, :]
            for kc in range(8):
                mm(o4, ggl[:, kc * 512 + cc:kc * 512 + cc + 128], uT[:, us, kc, :], kc == 0, kc == 7)
            act(tnh[:, 1, :], o4, AF.Tanh, bias=hbfm[:, 8 + fc:9 + fc], scale=0.5)
            pfree(o4)
            o1 = gps()[:, :]
            for kc in range(4):
                mm(o1, b11[:, kc * 1024 + fc * 128:kc * 1024 + (fc + 1) * 128], naoT[:, kc, :], kc == 0, kc == 3)
            m1 = tmpf[nxt("tmpf", 3)]
            stt("dve", m1[:, :], tnh[:, 0, :], 1.0, o1, ALU.add, ALU.mult)
            pfree(o1)
            o2 = gps()[:, :]
            for kc in range(4):
                mm(o2, b12[:, kc * 1024 + fc * 128:kc * 1024 + (fc + 1) * 128], glaoT[:, kc, :], kc == 0, kc == 3)
            m2 = tmpf[nxt("tmpf", 3)]
            stt("dve", m2[:, :], tnh[:, 1, :], 1.0, o2, ALU.add, ALU.mult)
            pfree(o2)
            tt("pool", mergedT[:, fc, :], m1[:, :], m2[:, :], ALU.add)
        w_release(p11)
        w_release(p12)
        w_release(p8)
        w_release(p10)
        pw13, bw13 = w_acquire(13)
        pw14, bw14 = w_acquire(14)
        for i in range(4):
            for half, bw in enumerate((bw13, bw14)):
                o = gps()[:, :]
                for kc in range(8):
                    mm(o, mergedT[:, kc, i * 128:(i + 1) * 128], bw[:, kc * 512:(kc + 1) * 512], kc == 0, kc == 7)
                hv = hbuf[:, i, half * 512:(half + 1) * 512]
                stt("dve", hv, o, 0.5, hv, ALU.mult, ALU.add)
                pfree(o)
            ssc = stat_cols(4)
            ss = ssc.sub((slice(None), slice(0, 1)))
            j = 0
            act(hn[:, j, :], hbuf[:, i, :], AF.Square, accum=ss)
            rs = rstd_from_ss(ss, 1, 1.0 / D)
            tsm("dve", hn[:, j, :], hbuf[:, i, :], rs)
            for k in range(8):
                tr(pT[:, k, :], hn[:, j, k * 128:(k + 1) * 128])
            tt("dve", uT[:, us, :, i * 128:(i + 1) * 128], pT[:, :, :], gfm[:, 8:16].bc(2, [128, 8, 128]), ALU.mult)
        w_release(pw13)
        w_release(pw14)
        if si_ in hoist_of:
            a_rest(*steps[hoist_of[si_]][1:])
        for q in range(4):
            pu0, bu0 = w_acquire(15 + 2 * q)
            pu1, bu1 = w_acquire(16 + 2 * q)
            pd0, bd0 = w_acquire(23 + 2 * q)
            pd1, bd1 = w_acquire(24 + 2 * q)
            hq = 0
            for fcl in range(8):
                bw = bu0 if fcl < 4 else bu1
                cc = (fcl % 4) * 128
                o = gps()[:, :]
                for kc in range(8):
                    mm(o, bw[:, kc * 512 + cc:kc * 512 + cc + 128], uT[:, us, kc, :], kc == 0, kc == 7)
                rj = 0
                act(rl[:, rj, :], o, AF.Relu)
                pfree(o)
                tt("pool", hid[:, hq, fcl, :], rl[:, rj, :], rl[:, rj, :], ALU.mult)
                if fcl == 3:
                    w_release(pu0)
            w_release(pu1)
            for i in range(4):
                for half in range(2):
                    o = gps()[:, :]
                    for kc in range(8):
                        bw = bd0 if kc < 4 else bd1
                        mm(o, hid[:, hq, kc, i * 128:(i + 1) * 128],
                           bw[:, (kc % 4) * 1024 + half * 512:(kc % 4) * 1024 + (half + 1) * 512], kc == 0, kc == 7)
                    hv = hbuf[:, i, half * 512:(half + 1) * 512]
                    tt("dve", hv, o, hv, ALU.add)
                    pfree(o)
            w_release(pd0)
            w_release(pd1)
        for i in range(4):
            ssc = stat_cols(4)
            ss = ssc.sub((slice(None), slice(0, 1)))
            act(tnh[:, :, :].re("p a c -> p (a c)"), hbuf[:, i, :], AF.Square, accum=ss)
            rs = rstd_from_ss(ss, 1, 1.0 / D)
            stt("dve", hbuf[:, i, :], hbuf[:, i, :], rs, gfin[:, :], ALU.mult, ALU.mult)
            r0 = tok0 + i * 128
            dma("sp", y_all[r0:r0 + 128, :], hbuf[:, i, :], f"y{i}")

    assert wstate["next"] == len(wseq)
    assert not plive, plive
    pr.emit(nc, es)
    print(f"[build] ops={len(pr.ops)} simulated_time_us={pr.sim_time / 1e3:.1f} busy_us=" +
          str({k: round(v / 1e3) for k, v in pr.sim_busy.items()}), flush=True)
    es.close()
    return nc, len(pr.ops)


_CACHE = {}


def kernel(**inputs):
    xp = np.asarray(inputs["x_prompt"], np.float32)
    xs = np.asarray(inputs["x_sample"], np.float32)
    bp, tp = xp.shape[0], xp.shape[1]
    bs, ts_ = xs.shape[0], xs.shape[1]
    npc, nsc = bp // NCORES, bs // NCORES
    seq_lens = [tp] * npc + [ts_] * nsc
    consts = _host_consts(inputs)
    key = tuple(seq_lens)
    if key not in _CACHE:
        _CACHE[key] = build_program(seq_lens)
    nc, _ = _CACHE[key]
    in_maps = []
    for c in range(NCORES):
        parts = [xp[c * npc + i] for i in range(npc)] + [xs[c * nsc + i] for i in range(nsc)]
        m = dict(consts)
        m["x_all"] = np.ascontiguousarray(np.concatenate(parts, 0))
        in_maps.append(m)
    res = run_bass_kernel_spmd(nc, in_maps, core_ids=list(range(NCORES)))
    yp = np.zeros_like(xp)
    ys = np.zeros_like(xs)
    for c in range(NCORES):
        y = res.results[c]["y_all"]
        o = 0
        for i in range(npc):
            yp[c * npc + i] = y[o:o + tp]
            o += tp
        for i in range(nsc):
            ys[c * nsc + i] = y[o:o + ts_]
            o += ts_
    return (yp, ys)
```

```python
import os
import numpy as np
from contextlib import ExitStack
import concourse.bass as bass
import concourse.mybir as mybir
from concourse.bass_utils import run_bass_kernel_spmd

F32 = mybir.dt.float32
BF16 = mybir.dt.bfloat16
AF = mybir.ActivationFunctionType
ALU = mybir.AluOpType
AX = mybir.AxisListType

D = 1024
NCORES = 8
NB = 31
BLK = 4096
RING = int(os.environ.get('K_RING', '4'))
EPS = 1e-6
NA_INTERLEAVE = os.environ.get('K_IL', '0') == '1'
KVS = int(os.environ.get('K_KVS', '12'))
C_NAQ, C_NAK, C_NAV = (0, 512), (512, 1024), (1024, 1536)
C_GQ, C_GK, C_GV, C_GG = (1536, 1792), (1792, 2048), (2048, 2560), (2560, 3072)
C_LRF, C_LRB = (3072, 3088), (3088, 3104)
C_GNA, C_GGLA = (3104, 4128), (4128, 5152)
EB_ORDER = [5, 4, 0, 1, 6, 2, 3, 7, 8]
NEB = len(EB_ORDER)


class View:
    __slots__ = ("buf", "ap", "lo", "hi")

    def __init__(self, buf, ap, lo, hi):
        self.buf, self.ap, self.lo, self.hi = buf, ap, lo, hi

    def re(self, pat, **kw):
        return View(self.buf, self.ap.rearrange(pat, **kw), self.lo, self.hi)

    def bc(self, axis, shape):
        return View(self.buf, self.ap.unsqueeze(axis).to_broadcast(list(shape)), self.lo, self.hi)

    def sub(self, key):
        return View(self.buf, self.ap[key], self.lo, self.hi)


class Buf:
    def __init__(self, name, t, shape, space, parent=None, base=0):
        self.name, self.t, self.shape, self.space = name, t, tuple(shape), space
        self.parent, self.base = parent, base
        self.bank_elems = 512
        n = len(shape)
        fs = [0] * n
        acc = 1
        for i in range(n - 1, 0, -1):
            fs[i] = acc
            acc *= shape[i]
        fs[0] = acc if space == "dram" else 0
        self.fs = fs
        self.wrecs = []
        self.rrecs = []

    @property
    def root(self):
        return self.parent if self.parent is not None else self

    def __getitem__(self, key):
        if not isinstance(key, tuple):
            key = (key,)
        key = key + (slice(None),) * (len(self.shape) - len(key))
        lo = hi = self.base
        for d, k in enumerate(key):
            n = self.shape[d]
            if isinstance(k, int):
                a = b = k
            else:
                a = 0 if k.start is None else k.start
                e = n if k.stop is None else k.stop
                st = 1 if k.step is None else k.step
                b = a + ((e - a - 1) // st) * st
            lo += a * self.fs[d]
            hi += b * self.fs[d]
        hi += 1
        if self.space == "psum":
            be = self.bank_elems
            lo = (lo // be) * be
            hi = -(-hi // be) * be
        return View(self, self.t[key], lo, hi)


class Op:
    __slots__ = ("eng", "fn", "deps", "sig", "sigval", "dma", "chan", "chanval", "idx", "dur", "lat", "fin", "nin", "succ")


class Prog:
    ENGS = ("pe", "act", "dve", "pool", "sp")

    def __init__(self):
        self.ops = []
        self.chan_count = {}

    def add(self, eng, fn, reads, writes, chan=None, dur=100.0, lat=None):
        op = Op()
        op.eng, op.fn, op.sig, op.sigval = eng, fn, False, 0
        op.dma = chan is not None
        op.chan = chan
        op.idx = len(self.ops)
        op.dur = dur
        op.lat = dur if lat is None else lat
        if op.dma:
            self.chan_count[chan] = self.chan_count.get(chan, 0) + 16
            op.chanval = self.chan_count[chan]
        deps = set()
        for v in reads:
            for r in v.buf.root.wrecs:
                if r[0] < v.hi and v.lo < r[1]:
                    deps.add(r[2])
        for v in writes:
            b = v.buf.root
            for r in b.wrecs:
                if r[0] < v.hi and v.lo < r[1]:
                    deps.add(r[2])
            for r in b.rrecs:
                if r[0] < v.hi and v.lo < r[1]:
                    deps.add(r[2])
        deps.discard(op)
        op.deps = deps
        for v in reads:
            v.buf.root.rrecs.append((v.lo, v.hi, op))
        for v in writes:
            b = v.buf.root
            b.wrecs = [r for r in b.wrecs if not (r[0] >= v.lo and r[1] <= v.hi)]
            b.rrecs = [r for r in b.rrecs if not (r[0] >= v.lo and r[1] <= v.hi)]
            b.wrecs.append((v.lo, v.hi, op))
        self.ops.append(op)
        return op

    def schedule(self):
        import heapq
        ops = self.ops
        for op in ops:
            op.succ = []
            op.nin = len(op.deps)
            op.fin = 0.0
        for op in ops:
            for d in op.deps:
                d.succ.append(op)
        prio_mode = os.environ.get("K_PRIO", "0.5")
        if prio_mode != "idx":
            bl = [0.0] * len(ops)
            for op in reversed(ops):
                m_ = 0.0
                for s_ in op.succ:
                    if bl[s_.idx] > m_:
                        m_ = bl[s_.idx]
                bl[op.idx] = m_ + op.lat
            tot = bl[0] if bl else 1.0
            w_ = float(prio_mode)
            scale = len(ops) / max(tot, 1.0)
            key_of = [op.idx - w_ * (bl[op.idx] * scale - (len(ops) - op.idx)) for op in ops]
        else:
            key_of = [float(op.idx) for op in ops]
        pending = {e: [] for e in self.ENGS}
        avail = {e: [] for e in self.ENGS}
        free = {e: 0.0 for e in self.ENGS}
        ready_t = {}
        for op in ops:
            if op.nin == 0:
                heapq.heappush(pending[op.eng], (0.0, op.idx))
        order = {e: [] for e in self.ENGS}
        left = len(ops)
        XLAT = 80.0
        while left:
            best = None
            for e in self.ENGS:
                pe_, av = pending[e], avail[e]
                t = free[e]
                while pe_ and pe_[0][0] <= t:
                    i_ = heapq.heappop(pe_)[1]
                    heapq.heappush(av, (key_of[i_], i_))
                if av:
                    cand = (t, av[0][1], e, True)
                elif pe_:
                    cand = (pe_[0][0], pe_[0][1], e, False)
                else:
                    continue
                if best is None or cand[:2] < best[:2]:
                    best = cand
            start, idx, e, from_av = best
            if from_av:
                heapq.heappop(avail[e])
            else:
                heapq.heappop(pending[e])
            op = ops[idx]
            order[e].append(op)
            free[e] = start + op.dur
            op.fin = start + op.lat
            left -= 1
            for s_ in op.succ:
                s_.nin -= 1
                rt = op.fin + (0.0 if (s_.eng == op.eng and not op.dma) else XLAT)
                if ready_t.get(s_.idx, 0.0) < rt:
                    ready_t[s_.idx] = rt
                if s_.nin == 0:
                    heapq.heappush(pending[s_.eng], (ready_t.get(s_.idx, 0.0), s_.idx))
        self.sim_time = max(op.fin for op in ops)
        self.sim_busy = {e: sum(op.dur for op in order[e]) for e in self.ENGS}
        return order

    def emit(self, nc, es):
        per = self.schedule()
        for op in self.ops:
            for d in op.deps:
                if (not d.dma) and not (d.eng == "pe" and op.eng == "pe"):
                    d.sig = True
        for e in self.ENGS:
            c = 0
            for op in per[e]:
                if (not op.dma) and op.sig:
                    c += 1
                    op.sigval = c
        esem = {e: es.enter_context(nc.semaphore("s_" + e)) for e in self.ENGS}
        csem = {c: es.enter_context(nc.semaphore("c_" + c)) for c in self.chan_count}
        handles = {"pe": "tensor", "act": "scalar", "dve": "vector", "pool": "gpsimd", "sp": "sync"}
        final_waits = [(csem[c], v) for c, v in self.chan_count.items()]

        def run(e, h):
            waited = {}
            for op in per[e]:
                need = {}
                deps_ = op.deps
                if len(deps_) > 1:
                    implied = set()
                    for d2 in deps_:
                        if d2.deps:
                            implied |= (d2.deps & deps_)
                    if implied:
                        deps_ = deps_ - implied
                for d in deps_:
                    if d.dma:
                        k, val = csem[d.chan], d.chanval
                    elif d.eng == "pe" and e == "pe":
                        continue
                    else:
                        k, val = esem[d.eng], d.sigval
                    if need.get(k, 0) < val:
                        need[k] = val
                todo = [(k, val) for k, val in need.items() if waited.get(k, 0) < val]
                for k, val in todo:
                    waited[k] = val
                attach = todo.pop() if (todo and e != "sp") else None
                for k, val in todo:
                    h.wait_ge(k, val)
                ins = op.fn(h)
                if attach is not None:
                    ins._wait_ge(attach[0], attach[1])
                if op.dma:
                    ins.then_inc(csem[op.chan], 16)
                elif op.sig:
                    ins.then_inc(esem[e], 1)
            if e == "sp":
                for k, val in final_waits:
                    if waited.get(k, 0) < val:
                        h.wait_ge(k, val)

        block = es.enter_context(nc.Block())
        block.tensor(lambda h: run("pe", h))
        block.scalar(lambda h: run("act", h))
        block.vector(lambda h: run("dve", h))
        block.gpsimd(lambda h: run("pool", h))
        block.sync(lambda h: run("sp", h))


def _blk(w, cols, nk, ncols):
    out = np.zeros((128, nk, ncols), np.float32)
    sub = w[:, cols].reshape(nk, 128, len(cols))
    out[:, :, : len(cols)] = sub.transpose(1, 0, 2)
    return out.reshape(128, nk * ncols)


def _na_variants():
    rows, kh = 64, 8
    rs = lambda r: int(np.clip(r - 4, 0, rows - kh))
    allv = {}
    for m in range(rows // 2):
        lo, hi = rs(2 * m), rs(2 * m + 1) + 8
        for kt in range(lo // 2, (hi - 1) // 2 + 1):
            key = []
            for krl in range(2):
                for qrl in range(2):
                    kr, qr = 2 * kt + krl, 2 * m + qrl
                    key.append(kr - qr + 7 if rs(qr) <= kr < rs(qr) + 8 else None)
            allv.setdefault(tuple(key), len(allv))
    return allv


def _na_tile_info(rows):
    kh = 8
    rs = lambda r: int(np.clip(r - 4, 0, rows - kh))
    allv = _na_variants()
    info = []
    for m in range(rows // 2):
        lo, hi = rs(2 * m), rs(2 * m + 1) + 8
        kts = list(range(lo // 2, (hi - 1) // 2 + 1))
        pat = []
        for kt in kts:
            key = []
            for krl in range(2):
                for qrl in range(2):
                    kr, qr = 2 * kt + krl, 2 * m + qrl
                    key.append(kr - qr + 7 if rs(qr) <= kr < rs(qr) + 8 else None)
            pat.append(allv[tuple(key)])
        pos = [EB_ORDER.index(v) for v in pat]
        runs = []
        a = 0
        for b in range(1, len(pos) + 1):
            if b == len(pos) or pos[b] != pos[b - 1] + 1:
                runs.append((a, b - a, pos[a]))
                a = b
        info.append((kts, runs))
    return info


def _na_bias_table(rpb):
    allv = _na_variants()
    inv = {v: k for k, v in allv.items()}
    c = np.arange(64)
    col_start = np.clip(c - 8, 0, 48)
    cmask = (c[None, :] >= col_start[:, None]) & (c[None, :] < col_start[:, None] + 16)
    dc = np.clip(c[None, :] - c[:, None], -15, 15) + 15
    tab = np.full((128, NEB, 8, 128), -30000.0, np.float32)
    for e, vid in enumerate(EB_ORDER):
        key = inv[vid]
        for krl in range(2):
            for qrl in range(2):
                dr = key[krl * 2 + qrl]
                if dr is None:
                    continue
                vals = rpb[:, dr, :][:, dc]
                vals = np.where(cmask[None], vals, np.float32(-30000.0))
                tab[krl * 64:(krl + 1) * 64, e, :, qrl * 64:(qrl + 1) * 64] = vals.transpose(2, 0, 1)
    return tab.reshape(128, NEB * 8 * 128)


def _host_consts(inp):
    w_in = np.asarray(inp["w_in"][0], np.float32)
    b_in = np.asarray(inp["b_in"][0], np.float32)
    ar = lambda r: np.arange(r[0], r[1])
    blocks = []
    blocks.append(_blk(w_in, ar(C_NAQ), 8, 512))
    blocks.append(_blk(w_in, ar(C_NAK), 8, 512))
    blocks.append(_blk(w_in, ar(C_NAV), 8, 512))
    blocks.append(_blk(w_in, np.concatenate([ar(C_GQ), ar(C_GK)]), 8, 512))
    blocks.append(_blk(w_in, ar(C_GV), 8, 512))
    blocks.append(_blk(w_in, ar(C_GG), 8, 512))
    blocks.append(_blk(w_in, np.concatenate([ar(C_GK), ar(C_LRF), ar(C_LRB)]), 8, 512))
    blocks.append(_blk(w_in, ar(C_GNA)[:512], 8, 512))
    blocks.append(_blk(w_in, ar(C_GNA)[512:], 8, 512))
    blocks.append(_blk(w_in, ar(C_GGLA)[:512], 8, 512))
    blocks.append(_blk(w_in, ar(C_GGLA)[512:], 8, 512))
    blocks.append(_blk(np.asarray(inp["w_br_na"][0], np.float32), np.arange(1024), 4, 1024))
    blocks.append(_blk(np.asarray(inp["w_br_gla"][0], np.float32), np.arange(1024), 4, 1024))
    w_out = np.asarray(inp["w_out"][0], np.float32)
    blocks.append(_blk(w_out, np.arange(0, 512), 8, 512))
    blocks.append(_blk(w_out, np.arange(512, 1024), 8, 512))
    w_up = np.asarray(inp["w_up"][0], np.float32)
    for j in range(8):
        blocks.append(_blk(w_up, np.arange(512 * j, 512 * (j + 1)), 8, 512))
    w_down = np.asarray(inp["w_down"][0], np.float32)
    for j in range(8):
        blocks.append(_blk(w_down[512 * j:512 * (j + 1)], np.arange(1024), 4, 1024))
    wsrc = np.stack(blocks, 0)
    assert wsrc.shape == (NB, 128, BLK)

    bfm = np.zeros((128, 40), np.float32)
    for ch in range(4):
        bfm[:, ch] = b_in[C_NAQ[0] + ch * 128: C_NAQ[0] + (ch + 1) * 128]
        bfm[:, 4 + ch] = b_in[C_NAK[0] + ch * 128: C_NAK[0] + (ch + 1) * 128]
    for h in range(4):
        bfm[:64, 8 + h] = b_in[C_GQ[0] + h * 64: C_GQ[0] + (h + 1) * 64]
        bfm[:64, 12 + h] = b_in[C_GK[0] + h * 64: C_GK[0] + (h + 1) * 64]
    bfm[:16, 16] = b_in[C_LRF[0]:C_LRF[1]]
    bfm[16:32, 16] = b_in[C_LRB[0]:C_LRB[1]]
    for ch in range(8):
        bfm[:, 17 + ch] = b_in[C_GNA[0] + ch * 128: C_GNA[0] + (ch + 1) * 128]
        bfm[:, 25 + ch] = b_in[C_GGLA[0] + ch * 128: C_GGLA[0] + (ch + 1) * 128]
    brow = np.zeros((33, 1024), np.float32)
    brow[0, 0:512] = b_in[C_NAV[0]:C_NAV[1]]
    brow[0, 512:1024] = b_in[C_GV[0]:C_GV[1]]
    brow[32, 0:512] = b_in[C_GG[0]:C_GG[1]]
    brow[32, 512:768] = b_in[C_GK[0]:C_GK[1]]
    gk = np.zeros((33, 512), np.float32)
    gk[0:16, 0:256] = np.asarray(inp["gk_fwd_w"][0], np.float32)
    gk[16:32, 256:512] = np.asarray(inp["gk_bwd_w"][0], np.float32)
    gk[32, 0:256] = np.asarray(inp["gk_fwd_b"][0], np.float32)
    gk[32, 256:512] = np.asarray(inp["gk_bwd_b"][0], np.float32)
    gfm = np.zeros((128, 16), np.float32)
    gfm[:, 0:8] = np.asarray(inp["norm_mix_g"][0], np.float32).reshape(8, 128).T
    gfm[:, 8:16] = np.asarray(inp["norm_mlp_g"][0], np.float32).reshape(8, 128).T
    gfin = np.broadcast_to(np.asarray(inp["norm_final_g"], np.float32)[None, :], (128, 1024)).copy()
    ggla = np.broadcast_to(np.asarray(inp["gla_norm_g"][0], np.float32)[None, :], (128, 128)).copy()
    j = np.arange(128)[:, None]
    t = np.arange(128)[None, :]
    s = np.float32(-1.0 / 16.0)
    tri = np.concatenate([(j <= t) * s, (j > t) * s, (j >= t) * s, (j < t) * s], 1).astype(np.float32)
    c16 = np.concatenate([np.eye(128), (j <= t) * 1.0, (j >= t) * 1.0, np.ones((128, 128))], 1).astype(np.float32)
    natab = _na_bias_table(np.asarray(inp["na_rpb"][0], np.float32))
    return dict(wsrc=wsrc, bfm=bfm, brow=brow, gk=gk, gfm=gfm, gfin=gfin, ggla=ggla, tri=tri, c16=c16, natab=natab)


def build_program(seq_lens):
    ntok = int(sum(seq_lens))
    nc = bass.Bass("TRN2", target_bir_lowering=False)
    es = ExitStack()
    pr = Prog()

    def dram(name, shape, dt, kind):
        return Buf(name, nc.dram_tensor(name, list(shape), dt, kind=kind).ap(), shape, "dram")

    x_all = dram("x_all", [ntok, D], F32, "ExternalInput")
    y_all = dram("y_all", [ntok, D], F32, "ExternalOutput")
    wsrc = dram("wsrc", [NB, 128, BLK], F32, "ExternalInput")
    d_bfm = dram("bfm", [128, 40], F32, "ExternalInput")
    d_brow = dram("brow", [33, 1024], F32, "ExternalInput")
    d_gk = dram("gk", [33, 512], F32, "ExternalInput")
    d_gfm = dram("gfm", [128, 16], F32, "ExternalInput")
    d_gfin = dram("gfin", [128, 1024], F32, "ExternalInput")
    d_ggla = dram("ggla", [128, 128], F32, "ExternalInput")
    d_tri = dram("tri", [128, 512], F32, "ExternalInput")
    d_c16 = dram("c16", [128, 512], F32, "ExternalInput")
    d_natab = dram("natab", [128, NEB * 1024], F32, "ExternalInput")
    wbf = dram("wbf", [NB, 128, BLK], BF16, "Internal")
    rst = dram("rst", [2, 32, 64, 512], BF16, "Internal")
    d_uT = dram("d_uT", [2, 8, 128, 4096], BF16, "Internal")
    d_vtm = dram("d_vtm", [2, 8, 128, 2048], BF16, "Internal")
    d_ktm = dram("d_ktm", [2, 8, 128, 1024], BF16, "Internal")
    d_lrT = dram("d_lrT", [2, 8, 32, 512], BF16, "Internal")

    def sb(name, shape, dt):
        return Buf(name, es.enter_context(nc.sbuf_tensor("sb_" + name, list(shape), dt)), shape, "sbuf")

    def ps(name, shape, dt):
        return Buf(name, es.enter_context(nc.psum_tensor("ps_" + name, list(shape), dt)), shape, "psum")

    wring = [sb(f"wr{i}", [128, BLK], BF16) for i in range(RING)]
    c16 = sb("c16", [128, 512], BF16)
    tri = sb("tri", [128, 512], F32)
    bfm = sb("bfm", [128, 40], F32)
    hbfm = sb("hbfm", [128, 16], F32)
    brow = sb("brow", [33, 1024], BF16)
    gkaug = sb("gkaug", [33, 512], BF16)
    gfm = sb("gfm", [128, 16], F32)
    gfin = sb("gfin", [128, 1024], F32)
    gglah = sb("gglah", [128, 128], F32)
    EB = sb("EB", [128, NEB, 8, 128], BF16)
    xst = sb("xst", [128, 1, 1024], F32)
    hn = sb("hn", [128, 1, 1024], BF16)
    stat = sb("stat", [128, 128], F32)
    uT = sb("uT", [128, 2, 8, 512], BF16)
    hbuf = sb("hbuf", [128, 4, 1024], F32)
    kT = sb("kT", [128, KVS, 4, 128], BF16)
    vr = sb("vr", [128, KVS, 8, 65], BF16)
    qT = sb("qT", [128, 4, 512], BF16)
    ar = sb("ar", [128, 4096], BF16)
    gqT = Buf("gqT", ar.t[0:64, 0:2048].rearrange("p (h t) -> p h t", h=4), [64, 4, 512], "sbuf", parent=ar, base=0)
    gkT = Buf("gkT", ar.t[0:64, 2048:4096].rearrange("p (h t) -> p h t", h=4), [64, 4, 512], "sbuf", parent=ar, base=2048)
    mergedT = Buf("mergedT", ar.t[:, :].rearrange("p (k t) -> p k t", k=8), [128, 8, 512], "sbuf", parent=ar, base=0)
    hid = Buf("hid", ar.t[:, :].rearrange("p (a k t) -> p a k t", a=1, k=8), [128, 1, 8, 512], "sbuf", parent=ar, base=0)
    v_tm = sb("v_tm", [128, 4, 512], BF16)
    sg = sb("sg", [128, 4, 512], BF16)
    k_tm = sb("k_tm", [128, 4, 256], BF16)
    lrT = sb("lrT", [33, 512], BF16)
    uTA = sb("uTA", [128, 1, 8, 512], BF16)
    vA = sb("vA", [128, 2, 512], BF16)
    kA = sb("kA", [128, 2, 256], BF16)
    lrTA = sb("lrTA", [33, 512], BF16)
    spA = sb("spA", [128, 256], F32)
    GexpA = sb("GexpA", [128, 256], F32)
    kddA = sb("kddA", [128, 256], BF16)
    tmpf = [sb(f"tmpf{i}", [128, 512], F32) for i in range(4)]
    e_t = tmpf[3]
    sp_t = sb("sp_t", [128, 1, 512], F32)
    E1f = sb("E1f", [64, 512], F32)
    E2f = sb("E2f", [64, 512], F32)
    E1b = E1f
    E2b = E2f
    Dfw = sb("Dfw", [64, 4], F32)
    Gexp = sb("Gexp", [128, 256], F32)
    Db = sb("Db", [64, 8], F32)
    qdf = sb("qdf", [64, 4, 128], BF16)
    kdf = sb("kdf", [64, 4, 128], BF16)
    qdb = sb("qdb", [64, 4, 128], BF16)
    kdb = sb("kdb", [64, 4, 128], BF16)
    kdd = sb("kdd", [128, 256], BF16)
    Amf = sb("Amf", [128, 4, 128], BF16)
    Amb = sb("Amb", [128, 4, 128], BF16)
    go = sb("go", [128, 512], BF16)
    S = sb("S", [64, 512], F32)
    Sbf = sb("Sbf", [64, 512], BF16)
    R = sb("R", [64, 512], F32)
    Rbf = sb("Rbf", [64, 1, 512], BF16)
    Rin = sb("Rin", [64, 1, 512], BF16)
    pexp = sb("pexp", [128, 2, 640], BF16)
    PTb = sb("PTb", [128, 2, 5, 128], BF16)
    nao = sb("nao", [128, 512], BF16)
    naoT = sb("naoT", [128, 4, 512], BF16)
    glaoT = sb("glaoT", [128, 4, 512], BF16)
    tnh = sb("tnh", [128, 2, 512], BF16)
    rl = sb("rl", [128, 1, 512], BF16)
    NPG = 5
    pg = [ps(f"pg{i}", [128, 512], F32) for i in range(NPG)]
    pS = [ps(f"pS{i}", [128, 512], F32) for i in range(2)]
    pT = ps("pT", [128, 8, 128], BF16)
    pT.bank_elems = 1024

    try:
        print("[build] sbuf bytes remaining per partition:", nc.sbuf_bytes_remaining // 128 if nc.sbuf_bytes_remaining > 300000 else nc.sbuf_bytes_remaining, flush=True)
    except Exception as ex:
        print("[build] sbuf remaining n/a", ex)
    cnt = {"va": 0, "pg": 0, "stat": 0, "xst": 0, "tmpf": 0, "pS": 0, "na": 0, "rb": 0, "ri": 0, "rl": 0, "tn": 0}

    def nxt(key, n):
        v = cnt[key]
        cnt[key] = v + 1
        return v % n

    pool_mode = {"A": False}

    pfree_list = list(range(NPG))
    plive = set()

    def gps(kind="d"):
        assert pfree_list, "out of PSUM banks (missing pfree?)"
        b_ = pfree_list.pop(0)
        plive.add(b_)
        return pg[b_]

    def pfree(x):
        b_ = x.buf if isinstance(x, View) else x
        i_ = pg.index(b_)
        assert i_ in plive, i_
        plive.discard(i_)
        pfree_list.append(i_)

    def stat_cols(n=4):
        c = nxt("stat", 16) * 8
        return stat[:, c:c + n]

    ident = c16[:, 0:128]
    Mf = c16[:, 128:256]
    Mb = c16[:, 256:384]
    ones_row = c16[0:1, 384:512]
    UTs, SLTs, LTs, SUTs = tri[:, 0:128], tri[:, 128:256], tri[:, 256:384], tri[:, 384:512]

    def nfree(v):
        n = 1
        for d_ in v.ap.shape[1:]:
            n *= int(d_)
        return n

    def is16(v):
        return v.ap.dtype == BF16 and v.buf.space == "sbuf"

    def vcost(eng, out, ins):
        n = nfree(out)
        if eng == "pool":
            return 150.0 + 1.7 * n
        f = 0.55 if (is16(out) and all(is16(i) for i in ins)) else 1.04
        return 150.0 + f * n

    def mm(out, lhsT, rhs, start, stop):
        n = nfree(rhs) * (4 if rhs.ap.dtype == F32 else 1)
        dur = n / 2.05 + (85.0 if n < 256 else 5.0)
        pr.add("pe", lambda e, o=out.ap, l=lhsT.ap, r=rhs.ap, s0=start, s1=stop:
               e.matmul(o, lhsT=l, rhs=r, start=s0, stop=s1), [lhsT, rhs], [out], dur=dur, lat=dur + 120.0)

    def tr(out, in_):
        pr.add("pe", lambda e, o=out.ap, i=in_.ap, d=ident.ap: e.transpose(o, i, d), [in_, ident], [out],
               dur=160.0, lat=300.0)

    def act(out, in_, func, bias=None, scale=1.0, accum=None):
        reads = [in_]
        kw = {}
        if isinstance(bias, View):
            reads.append(bias)
            kw["bias"] = bias.ap
        elif bias is not None:
            kw["bias"] = float(bias)
        if isinstance(scale, View):
            reads.append(scale)
            kw["scale"] = scale.ap
        else:
            kw["scale"] = float(scale)
        writes = [out]
        if accum is not None:
            writes.append(accum)
            kw["accum_out"] = accum.ap
        d_ = 200.0 + 0.83 * nfree(out)
        pr.add("act", lambda e, o=out.ap, i=in_.ap, f=func, kw=kw: e.activation(out=o, in_=i, func=f, **kw),
               reads, writes, dur=d_, lat=d_ + 60.0)

    def _h(e, name):
        return getattr(e, name)

    def tt(eng, out, in0, in1, op):
        d_ = vcost(eng, out, [in0, in1])
        pr.add(eng, lambda e, o=out.ap, a=in0.ap, b=in1.ap, op=op: e.tensor_tensor(o, a, b, op), [in0, in1], [out],
               dur=d_, lat=d_ + 60.0)

    def ts(eng, out, in0, s1, s2, op0, op1):
        reads = [in0]
        a1 = s1.ap if isinstance(s1, View) else float(s1)
        a2 = s2.ap if isinstance(s2, View) else float(s2)
        reads += [s for s in (s1, s2) if isinstance(s, View)]
        d_ = vcost(eng, out, [in0])
        pr.add(eng, lambda e, o=out.ap, a=in0.ap, a1=a1, a2=a2, op0=op0, op1=op1:
               e.tensor_scalar(o, a, a1, a2, op0, op1), reads, [out], dur=d_, lat=d_ + 60.0)

    def tsm(eng, out, in0, s1):
        reads = [in0] + ([s1] if isinstance(s1, View) else [])
        a1 = s1.ap if isinstance(s1, View) else float(s1)
        d_ = vcost(eng, out, [in0])
        pr.add(eng, lambda e, o=out.ap, a=in0.ap, a1=a1: e.tensor_scalar_mul(o, a, a1), reads, [out], dur=d_, lat=d_ + 60.0)

    def stt(eng, out, in0, sc, in1, op0, op1):
        reads = [in0, in1] + ([sc] if isinstance(sc, View) else [])
        a = sc.ap if isinstance(sc, View) else float(sc)
        d_ = vcost(eng, out, [in0, in1])
        pr.add(eng, lambda e, o=out.ap, i0=in0.ap, a=a, i1=in1.ap, op0=op0, op1=op1:
               e.scalar_tensor_tensor(o, i0, a, i1, op0, op1), reads, [out], dur=d_, lat=d_ + 60.0)

    def cp(eng, out, in_):
        if eng == "act":
            d_ = 200.0 + 0.83 * nfree(out)
            pr.add("act", lambda e, o=out.ap, i=in_.ap: e.copy(o, i), [in_], [out], dur=d_, lat=d_ + 60.0)
        else:
            d_ = vcost(eng, out, [in_])
            pr.add(eng, lambda e, o=out.ap, i=in_.ap: e.tensor_copy(o, i), [in_], [out], dur=d_, lat=d_ + 60.0)

    def red(out, in_, op=ALU.add):
        d_ = 150.0 + 1.04 * nfree(in_)
        pr.add("dve", lambda e, o=out.ap, i=in_.ap, op=op: e.tensor_reduce(o, i, AX.X, op), [in_], [out], dur=d_, lat=d_ + 60.0)

    def recip(out, in_):
        pr.add("dve", lambda e, o=out.ap, i=in_.ap: e.reciprocal(o, i), [in_], [out], dur=200.0, lat=260.0)

    def memset(eng, v, val):
        pr.add(eng, lambda e, o=v.ap, val=val: e.memset(o, val), [], [v], dur=150.0 + nfree(v), lat=200.0 + nfree(v))

    def dma(q, out, in_, chan):
        nb = 1
        for d_ in out.ap.shape:
            nb *= int(d_)
        nb *= 2 if out.ap.dtype == BF16 else 4
        pr.add(q, lambda e, o=out.ap, i=in_.ap: e.dma_start(out=o, in_=i), [in_], [out], chan=chan,
               dur=60.0, lat=2200.0 + nb / 150.0)

    for i in range(0, NB, 4):
        j = min(NB, i + 4)
        dma("pool", wbf[i:j], wsrc[i:j], f"wc{i}")
    dma("pool", c16[:, :], d_c16[:, :], "k0")
    dma("sp", tri[:, :], d_tri[:, :], "k1")
    dma("sp", bfm[:, :], d_bfm[:, :], "k2")
    dma("pool", brow[:, :], d_brow[:, :], "k3")
    dma("pool", gkaug[:, :], d_gk[:, :], "k4")
    dma("sp", gfm[:, :], d_gfm[:, :], "k5")
    dma("sp", gfin[:, :], d_gfin[:, :], "k6")
    dma("sp", gglah[:, :], d_ggla[:, :], "k7")
    ts("dve", gglah[:, :], gglah[:, :], 0.5, 0.0, ALU.mult, ALU.add)
    ts("dve", hbfm[:, :], bfm[:, 17:33], 0.5, 0.0, ALU.mult, ALU.add)
    memset("pool", vr[:, :, :, :], 1.0)
    memset("pool", lrT[:, :], 1.0)
    memset("pool", lrTA[:, :], 1.0)
    for e in range(NEB):
        t0 = tmpf[e % 2]
        t1 = tmpf[2]
        for hf in range(2):
            dma("sp", t0[:, :], d_natab[:, e * 1024 + hf * 512: e * 1024 + (hf + 1) * 512], f"nt{e % 2}")
            act(t1[:, :], t0[:, :], AF.Exp)
            cp("dve", EB[:, e, hf * 4:(hf + 1) * 4, :], t1[:, :].re("p (h q) -> p h q", h=4))

    def step_blocks(mode):
        if mode == "A":
            return [4, 6]
        return [1, 2, 0, 3, 5, 7, 9, 11, 12, 8, 10, 13, 14,
                15, 16, 23, 24, 17, 18, 25, 26, 19, 20, 27, 28, 21, 22, 29, 30]

    steps = []
    t0 = 0
    seqs_ = []
    for si, T in enumerate(seq_lens):
        seqs_.append((t0, T, si % 2))
        t0 += T
    def a_steps(q):
        return [("A", q[0], q[1], g, q[2]) for g in reversed(range(q[1] // 512))]
    def b_steps(q):
        return [("B", q[0], q[1], g, q[2]) for g in range(q[1] // 512)]
    steps += a_steps(seqs_[0])
    for si, q in enumerate(seqs_):
        bs = b_steps(q)
        as_ = a_steps(seqs_[si + 1]) if si + 1 < len(seqs_) else []
        nB, nA = len(bs), len(as_)
        for j, st_ in enumerate(bs):
            steps.append(st_)
            steps += as_[(j * nA) // nB:((j + 1) * nA) // nB]
    B_PRE = [1, 2, 0, 3, 5, 7, 9, 11, 12, 8, 10, 13, 14]
    B_MLP = [15, 16, 23, 24, 17, 18, 25, 26, 19, 20, 27, 28, 21, 22, 29, 30]
    hoist_of = {}
    for i_, st_ in enumerate(steps):
        if st_[0] == "B" and i_ + 1 < len(steps) and steps[i_ + 1][0] == "A":
            hoist_of[i_] = i_ + 1
    hoisted_all = set(hoist_of.values())
    wseq = []
    for i_, st_ in enumerate(steps):
        if st_[0] == "A":
            if i_ not in hoisted_all:
                wseq += [4, 6]
        else:
            wseq += B_PRE + ([4, 6] if i_ in hoist_of else []) + B_MLP
    wstate = {"next": 0, "loaded": set()}

    def w_load(pos):
        wstate["loaded"].add(pos)
        if pos < len(wseq):
            b = wseq[pos]
            dma("sp", wring[pos % RING][:, :], wbf[b], f"w{pos % RING}")

    for p_ in range(RING):
        w_load(p_)

    def w_acquire(expect):
        pos = wstate["next"]
        wstate["next"] = pos + 1
        assert wseq[pos] == expect, (pos, wseq[pos], expect)
        assert pos in wstate["loaded"], pos
        return pos, wring[pos % RING]

    def w_release(pos):
        w_load(pos + RING)

    def rstd_from_ss(ssv, n, inv_n):
        c = stat_cols(8)
        assert n <= 4
        act(c.sub((slice(None), slice(0, n))), ssv, AF.Ln, bias=EPS, scale=inv_n)
        act(c.sub((slice(None), slice(4, 4 + n))), c.sub((slice(None), slice(0, n))), AF.Exp, scale=-0.5)
        return c.sub((slice(None), slice(4, 4 + n)))

    def prep(tok0, ubuf, uslot, gcol):
        for i in range(4):
            j = 0
            r0 = tok0 + i * 128
            dma("sp", xst[:, j, :], x_all[r0:r0 + 128, :], f"xs{j}")
            ssc = stat_cols(4)
            ss = ssc.sub((slice(None), slice(0, 1)))
            act(hn[:, j, :], xst[:, j, :], AF.Square, accum=ss)
            rs = rstd_from_ss(ss, 1, 1.0 / D)
            tsm("dve", hn[:, j, :], xst[:, j, :], rs)
            for k in range(8):
                tr(pT[:, k, :], hn[:, j, k * 128:(k + 1) * 128])
            tt("dve", ubuf[:, uslot, :, i * 128:(i + 1) * 128], pT[:, :, :],
               gfm[:, gcol:gcol + 8].bc(2, [128, 8, 128]), ALU.mult)

    def proj_tm(blk, ubuf, uslot, i, ncols, brow_row, boff, c0=0):
        p = gps()
        o = p[:, 0:ncols]
        for k in range(8):
            mm(o, ubuf[:, uslot, k, i * 128:(i + 1) * 128], blk[:, k * 512 + c0:k * 512 + c0 + ncols], k == 0, False)
        mm(o, c16[brow_row:brow_row + 1, 384:512], brow[brow_row:brow_row + 1, boff:boff + ncols], False, True)
        return o

    def proj_fm(blk, uslot, c0, M, ubuf=None):
        ubuf = uT if ubuf is None else ubuf
        p = gps()
        o = p[0:M, 0:512]
        for k in range(8):
            mm(o, blk[:, k * 512 + c0:k * 512 + c0 + M], ubuf[:, uslot, k, :], k == 0, k == 7)
        return o

    def gla_common_A(i, c, nt, par, g, b4, b6):
        tc = slice(i * 128, (i + 1) * 128)
        j = nxt("va", 2)
        o = proj_tm(b4, uTA, 0, i, 512, 0, 512)
        cp("dve", vA[:, j, :], o)
        pfree(o)
        o = proj_tm(b6, uTA, 0, i, 256, 32, 512)
        cp("act", kA[:, j, :], o)
        pfree(o)
        dma("sp", d_vtm[par, g, :, i * 512:(i + 1) * 512], vA[:, j, :], f"sv{j}")
        dma("sp", d_ktm[par, g, :, i * 256:(i + 1) * 256], kA[:, j, :], f"sk{j}")
        pz = gps()
        mm(pz[:, 0:256], lrTA[0:33, tc], gkaug[0:33, 256:512], True, True)
        act(e_t[:, 0:256], pz[:, 0:256], AF.Exp, scale=-1.0)
        pfree(pz)
        spv = spA[:, :]
        act(spv, e_t[:, 0:256], AF.Ln, bias=1.0)
        pgm = gps()
        mm(pgm[:, 0:256], SUTs, spv, True, True)
        act(GexpA[:, :], pgm[:, 0:256], AF.Exp)
        pfree(pgm)
        tt("dve", kddA[:, :], kA[:, j, :], GexpA[:, :], ALU.mult)
        pu = gps()
        for h in range(4):
            mm(pu[0:64, h * 128:(h + 1) * 128], kddA[:, h * 64:(h + 1) * 64], vA[:, j, h * 128:(h + 1) * 128], True, True)
        if c == nt - 1:
            cp("dve", R[:, :], pu[0:64, :])
        else:
            ptot = gps()
            for h in range(4):
                mm(ptot[0:64, h:h + 1], spv.sub((slice(None), slice(h * 64, (h + 1) * 64))), LTs.sub((slice(None), slice(0, 1))), True, True)
            dcol = (nxt("rb", 2)) * 4
            act(Db[:, dcol:dcol + 4], ptot[0:64, 0:4], AF.Exp)
            pfree(ptot)
            for h in range(4):
                stt("dve", R[:, h * 128:(h + 1) * 128], R[:, h * 128:(h + 1) * 128], Db[:, dcol + h:dcol + h + 1],
                    pu[0:64, h * 128:(h + 1) * 128], ALU.mult, ALU.add)
        pfree(pu)
        if c >= 1:
            cp("dve", Rbf[:, 0, :], R[:, :])
            dma("sp", rst[par, c - 1], Rbf[:, 0, :], "rs0")

    def gla_tile_B(i, c, nt, par):
        tc = slice(i * 128, (i + 1) * 128)
        if c < nt - 1:
            rj = 0
            dma("sp", Rin[:, rj, :], rst[par, c], f"ri{rj}")
        pz = gps("g")
        mm(pz[:, 0:512], lrT[0:33, tc], gkaug[0:33, 0:512], True, True)
        act(e_t[:, :], pz[:, 0:512], AF.Exp, scale=-1.0)
        pfree(pz)
        spv = sp_t[:, 0, :]
        act(spv, e_t[:, :], AF.Ln, bias=1.0)
        pc = gps("g")
        for h in range(4):
            mm(pc[0:64, h * 128:(h + 1) * 128], spv.sub((slice(None), slice(h * 64, (h + 1) * 64))), UTs, True, True)
        r4 = lambda b: b[:, :].re("p (h t) -> p h t", h=4)
        act(E1f[:, :], pc[0:64, :], AF.Exp)
        act(E2f[:, :], pc[0:64, :], AF.Exp, scale=-1.0)
        pfree(pc)
        tt("dve", qdf[:, :, :], gqT[:, :, tc], r4(E1f), ALU.mult)
        tt("dve", kdf[:, :, :], gkT[:, :, tc], r4(E2f), ALU.mult)
        cp("dve", Dfw[:, :], r4(E1f).sub((slice(None), slice(None), 127)))
        pcb = gps("g")
        for h in range(4):
            mm(pcb[0:64, h * 128:(h + 1) * 128], spv.sub((slice(None), slice(256 + h * 64, 256 + (h + 1) * 64))), LTs, True, True)
        act(E1b[:, :], pcb[0:64, :], AF.Exp)
        act(E2b[:, :], pcb[0:64, :], AF.Exp, scale=-1.0)
        pfree(pcb)
        tt("dve", qdb[:, :, :], gqT[:, :, tc], r4(E1b), ALU.mult)
        tt("dve", kdb[:, :, :], gkT[:, :, tc], r4(E2b), ALU.mult)
        pa = gps("g")
        for h in range(4):
            mm(pa[:, h * 128:(h + 1) * 128], kdf[:, h, :], qdf[:, h, :], True, True)
        tt("dve", Amf[:, :, :], pa[:, :].re("p (h t) -> p h t", h=4), Mf.bc(1, [128, 4, 128]), ALU.mult)
        pfree(pa)
        pab = gps("g")
        for h in range(4):
            mm(pab[:, h * 128:(h + 1) * 128], kdb[:, h, :], qdb[:, h, :], True, True)
        tt("dve", Amb[:, :, :], pab[:, :].re("p (h t) -> p h t", h=4), Mb.bc(1, [128, 4, 128]), ALU.mult)
        pfree(pab)
        po = gps("g")
        for h in range(4):
            hs = slice(h * 128, (h + 1) * 128)
            seq_ = [(Amf[:, h, :], v_tm[:, i, hs]), (Amb[:, h, :], v_tm[:, i, hs])]
            if c > 0:
                seq_.append((qdf[:, h, :], Sbf[:, hs]))
            if c < nt - 1:
                seq_.append((qdb[:, h, :], Rin[:, 0, hs]))
            for n_, (l_, r_) in enumerate(seq_):
                mm(po[:, hs], l_, r_, n_ == 0, n_ == len(seq_) - 1)
        if c < nt - 1:
            pgm = gps("g")
            mm(pgm[:, 0:256], SLTs, spv.sub((slice(None), slice(0, 256))), True, True)
            act(Gexp[:, :], pgm[:, 0:256], AF.Exp)
            pfree(pgm)
            tt("dve", kdd[:, :], k_tm[:, i, :], Gexp[:, :], ALU.mult)
            pu = gps("g")
            for h in range(4):
                mm(pu[0:64, h * 128:(h + 1) * 128], kdd[:, h * 64:(h + 1) * 64], v_tm[:, i, h * 128:(h + 1) * 128], True, True)
            if c == 0:
                cp("dve", S[:, :], pu[0:64, :])
            else:
                for h in range(4):
                    stt("dve", S[:, h * 128:(h + 1) * 128], S[:, h * 128:(h + 1) * 128],
                        Dfw[:, h:h + 1], pu[0:64, h * 128:(h + 1) * 128], ALU.mult, ALU.add)
            pfree(pu)
            cp("dve", Sbf[:, :], S[:, :])
        sq = tmpf[nxt("tmpf", 3)]
        act(sq[:, :], po[:, :], AF.Square)
        ssc = stat_cols(4)
        red(ssc, sq[:, :].re("p (h v) -> p h v", h=4))
        rs = rstd_from_ss(ssc, 4, 1.0 / 128)
        for h in range(4):
            hs = slice(h * 128, (h + 1) * 128)
            stt("dve", go[:, hs], po[:, hs], rs.sub((slice(None), slice(h, h + 1))), sg[:, i, hs], ALU.mult, ALU.mult)
        pfree(po)
        for k in range(4):
            tr(pT[:, k, :], go[:, k * 128:(k + 1) * 128])
        cp("act", glaoT[:, :, tc], pT[:, 0:4, :])

    def na_tile(i, m, info):
        tc = slice(i * 128, (i + 1) * 128)
        kts, runs = info[m]
        n = len(kts)
        nmain = min(n, 4)
        for hh in range(2):
            po = gps("n")
            for pp in range(2):
                he, ho = hh * 4 + pp * 2, hh * 4 + pp * 2 + 1
                p5 = {he: gps("n"), ho: gps("n")} if n == 5 else None

                def smm(h, idx):
                    prr, pb = h // 2, (h % 2) * 64
                    kt = kts[idx]
                    if idx < 4:
                        o_ = pS[h % 2][:, idx * 128:(idx + 1) * 128]
                    else:
                        o_ = p5[h][:, 0:128]
                    mm(o_, kT[pb:pb + 64, kt % KVS, prr, :], qT[pb:pb + 64, prr, tc], True, True)

                if NA_INTERLEAVE:
                    if n == 5:
                        smm(he, 4)
                    for idx in range(nmain):
                        smm(ho, idx)
                        smm(he, idx)
                    if n == 5:
                        smm(ho, 4)
                else:
                    for h in (he, ho):
                        for idx in range(n):
                            smm(h, idx)
                for h in (he, ho):
                    act(pexp[:, h % 2, 0:nmain * 128], pS[h % 2][:, 0:nmain * 128], AF.Exp, scale=0.125)
                if n == 5:
                    for h in (he, ho):
                        act(pexp[:, h % 2, 512:640], p5[h][:, 0:128], AF.Exp, scale=0.125)
                        pfree(p5[h])
                for h in (he, ho):
                    j = h % 2
                    for (a0, ln, p0) in runs:
                        tt("dve", PTb[:, j, a0:a0 + ln, :], pexp[:, j, a0 * 128:(a0 + ln) * 128].re("p (c q) -> p c q", c=ln),
                           EB[:, p0:p0 + ln, h, :], ALU.mult)
                for h in (he, ho):
                    j = h % 2
                    hl = h % 4
                    for idx, kt in enumerate(kts):
                        mm(po[:, hl * 65:(hl + 1) * 65], PTb[:, j, idx, :], vr[:, kt % KVS, h, :], idx == 0, idx == n - 1)
            ssc = stat_cols(4)
            pov = po[:, 0:260].re("p (h e) -> p h e", h=4)
            recip(ssc, pov.sub((slice(None), slice(None), 64)))
            tt("dve", nao[:, hh * 256:(hh + 1) * 256].re("p (h e) -> p h e", h=4),
               pov.sub((slice(None), slice(None), slice(0, 64))), ssc.bc(2, [128, 4, 64]), ALU.mult)
            pfree(po)
        for k in range(4):
            tr(pT[:, 4 + k, :], nao[:, k * 128:(k + 1) * 128])
        cp("act", naoT[:, :, tc], pT[:, 4:8, :])

    prepped = {}

    hoisted = set()
    for si_, (mode, seq_t0, T, g, par) in enumerate(steps):
        nt = T // 128
        ng = T // 512
        tok0 = seq_t0 + g * 512
        def a_prep(a_t0, a_T, a_g, a_par):
            pool_mode["A"] = True
            prep(a_t0 + a_g * 512, uTA, 0, 0)
            dma("sp", d_uT[a_par, a_g], uTA[:, 0, :, :].re("p k t -> p (k t)"), "su")
            pool_mode["A"] = False

        def a_rest(a_t0, a_T, a_g, a_par):
            pool_mode["A"] = True
            p4, b4 = w_acquire(4)
            p6, b6 = w_acquire(6)
            o = proj_fm(b6, 0, 256, 32, ubuf=uTA)
            act(lrTA[0:32, :], o, AF.Identity, bias=bfm[0:32, 16:17])
            pfree(o)
            dma("sp", d_lrT[a_par, a_g], lrTA[0:32, :], "sl")
            for i in reversed(range(4)):
                gla_common_A(i, a_g * 4 + i, a_T // 128, a_par, a_g, b4, b6)
            w_release(p4)
            w_release(p6)
            pool_mode["A"] = False

        if mode == "A":
            if si_ in hoisted_all:
                continue
            a_prep(seq_t0, T, g, par)
            a_rest(seq_t0, T, g, par)
            continue
        if si_ in hoist_of:
            a_prep(*steps[hoist_of[si_]][1:])
        info = _na_tile_info(T // 64)
        kvg = [0, 1] if g == 0 else ([g + 1] if g + 1 < ng else [])
        kvg = [x for x in kvg if x < ng]
        for gg in sorted(set([g] + kvg)):
            key = (seq_t0, gg, "B")
            if key not in prepped:
                prepped[key] = gg % 2
                dma("sp", uT[:, gg % 2, :, :].re("p k t -> p (k t)"), d_uT[par, gg], f"lu{gg % 2}")
        us = g % 2
        dma("sp", v_tm[:, :, :].re("p i c -> p (i c)"), d_vtm[par, g], "lv")
        dma("sp", k_tm[:, :, :].re("p i c -> p (i c)"), d_ktm[par, g], "lk")
        dma("sp", lrT[0:32, :], d_lrT[par, g], "ll")
        for i in range(4):
            r0 = tok0 + i * 128
            dma("sp", hbuf[:, i, :], x_all[r0:r0 + 128, :], f"h{i}")
        p1, b1 = w_acquire(1)
        for gg in kvg:
            for ch in range(4):
                o = proj_fm(b1, gg % 2, ch * 128, 128)
                s0 = (gg * 4) % KVS
                act(kT[:, s0:s0 + 4, ch, :], o.re("p (t k) -> p t k", t=4), AF.Identity, bias=bfm[:, 4 + ch:5 + ch])
                pfree(o)
        w_release(p1)
        p2, b2 = w_acquire(2)
        for gg in kvg:
            for i in range(4):
                o = proj_tm(b2, uT, gg % 2, i, 512, 0, 0)
                cp("act", vr[:, (gg * 4 + i) % KVS, :, 0:64], o.re("p (h e) -> p h e", h=8))
                pfree(o)
        w_release(p2)
        p0, b0 = w_acquire(0)
        for ch in range(4):
            o = proj_fm(b0, us, ch * 128, 128)
            act(qT[:, ch, :], o, AF.Identity, bias=bfm[:, ch:ch + 1])
            pfree(o)
        w_release(p0)
        p3, b3 = w_acquire(3)
        for c2 in range(2):
            o = proj_fm(b3, us, c2 * 128, 128)
            for hh_ in range(2):
                h = 2 * c2 + hh_
                ts("dve", gqT[:, h, :], o.buf[hh_ * 64:(hh_ + 1) * 64, 0:512], bfm[0:64, 8 + h:9 + h], 0.125, ALU.add, ALU.mult)
            pfree(o)
        for c2 in range(2):
            o = proj_fm(b3, us, 256 + c2 * 128, 128)
            for hh_ in range(2):
                h = 2 * c2 + hh_
                act(gkT[:, h, :], o.buf[hh_ * 64:(hh_ + 1) * 64, 0:512], AF.Identity, bias=bfm[0:64, 12 + h:13 + h])
            pfree(o)
        w_release(p3)
        p5, b5 = w_acquire(5)
        for i in range(4):
            o = proj_tm(b5, uT, us, i, 512, 32, 0)
            tv = tmpf[nxt("tmpf", 3)]
            act(tv[:, :], o, AF.Tanh, scale=0.5)
            tv2 = tmpf[nxt("tmpf", 3)]
            stt("dve", tv2[:, :], tv[:, :], 1.0, o, ALU.add, ALU.mult)
            pfree(o)
            tt("pool", sg[:, i, :].re("p (h v) -> p h v", h=4), tv2[:, :].re("p (h v) -> p h v", h=4),
               gglah[:, :].bc(1, [128, 4, 128]), ALU.mult)
        w_release(p5)
        for i in range(4):
            c = g * 4 + i
            gla_tile_B(i, c, nt, par)
            na_tile(i, c, info)
        p7, b7 = w_acquire(7)
        p9, b9 = w_acquire(9)
        p11, b11 = w_acquire(11)
        p12, b12 = w_acquire(12)
        p8 = p10 = None
        for fc in range(8):
            if fc == 4:
                w_release(p7)
                w_release(p9)
                p8, b8 = w_acquire(8)
                p10, b10 = w_acquire(10)
            gna = b7 if fc < 4 else b8
            ggl = b9 if fc < 4 else b10
            cc = (fc % 4) * 128
            o3 = gps()[:, :]
            for kc in range(8):
                mm(o3, gna[:, kc * 512 + cc:kc * 512 + cc + 128], uT[:, us, kc, :], kc == 0, kc == 7)
            act(tnh[:, 0, :], o3, AF.Tanh, bias=hbfm[:, fc:fc + 1], scale=0.5)
            pfree(o3)
            o4 = gps()[:, :]
            for kc in range(8):
                mm(o4, ggl[:, kc * 512 + cc:kc * 512 + cc + 128], uT[:, us, kc, :], kc == 0, kc == 7)
            act(tnh[:, 1, :], o4, AF.Tanh, bias=hbfm[:, 8 + fc:9 + fc], scale=0.5)
            pfree(o4)
            o1 = gps()[:, :]
            for kc in range(4):
                mm(o1, b11[:, kc * 1024 + fc * 128:kc * 1024 + (fc + 1) * 128], naoT[:, kc, :], kc == 0, kc == 3)
            m1 = tmpf[nxt("tmpf", 3)]
            stt("dve", m1[:, :], tnh[:, 0, :], 1.0, o1, ALU.add, ALU.mult)
            pfree(o1)
            o2 = gps()[:, :]
            for kc in range(4):
                mm(o2, b12[:, kc * 1024 + fc * 128:kc * 1024 + (fc + 1) * 128], glaoT[:, kc, :], kc == 0, kc == 3)
            m2 = tmpf[nxt("tmpf", 3)]
            stt("dve", m2[:, :], tnh[:, 1, :], 1.0, o2, ALU.add, ALU.mult)
            pfree(o2)
            tt("pool", mergedT[:, fc, :], m1[:, :], m2[:, :], ALU.add)
        w_release(p11)
        w_release(p12)
        w_release(p8)
        w_release(p10)
        pw13, bw13 = w_acquire(13)
        pw14, bw14 = w_acquire(14)
        for i in range(4):
            for half, bw in enumerate((bw13, bw14)):
                o = gps()[:, :]
                for kc in range(8):
                    mm(o, mergedT[:, kc, i * 128:(i + 1) * 128], bw[:, kc * 512:(kc + 1) * 512], kc == 0, kc == 7)
                hv = hbuf[:, i, half * 512:(half + 1) * 512]
                stt("dve", hv, o, 0.5, hv, ALU.mult, ALU.add)
                pfree(o)
            ssc = stat_cols(4)
            ss = ssc.sub((slice(None), slice(0, 1)))
            j = 0
            act(hn[:, j, :], hbuf[:, i, :], AF.Square, accum=ss)
            rs = rstd_from_ss(ss, 1, 1.0 / D)
            tsm("dve", hn[:, j, :], hbuf[:, i, :], rs)
            for k in range(8):
                tr(pT[:, k, :], hn[:, j, k * 128:(k + 1) * 128])
            tt("dve", uT[:, us, :, i * 128:(i + 1) * 128], pT[:, :, :], gfm[:, 8:16].bc(2, [128, 8, 128]), ALU.mult)
        w_release(pw13)
        w_release(pw14)
        if si_ in hoist_of:
            a_rest(*steps[hoist_of[si_]][1:])
        for q in range(4):
            pu0, bu0 = w_acquire(15 + 2 * q)
            pu1, bu1 = w_acquire(16 + 2 * q)
            pd0, bd0 = w_acquire(23 + 2 * q)
            pd1, bd1 = w_acquire(24 + 2 * q)
            hq = 0
            for fcl in range(8):
                bw = bu0 if fcl < 4 else bu1
                cc = (fcl % 4) * 128
                o = gps()[:, :]
                for kc in range(8):
                    mm(o, bw[:, kc * 512 + cc:kc * 512 + cc + 128], uT[:, us, kc, :], kc == 0, kc == 7)
                rj = 0
                act(rl[:, rj, :], o, AF.Relu)
                pfree(o)
                tt("pool", hid[:, hq, fcl, :], rl[:, rj, :], rl[:, rj, :], ALU.mult)
                if fcl == 3:
                    w_release(pu0)
            w_release(pu1)
            for i in range(4):
                for half in range(2):
                    o = gps()[:, :]
                    for kc in range(8):
                        bw = bd0 if kc < 4 else bd1
                        mm(o, hid[:, hq, kc, i * 128:(i + 1) * 128],
                           bw[:, (kc % 4) * 1024 + half * 512:(kc % 4) * 1024 + (half + 1) * 512], kc == 0, kc == 7)
                    hv = hbuf[:, i, half * 512:(half + 1) * 512]
                    tt("dve", hv, o, hv, ALU.add)
                    pfree(o)
            w_release(pd0)
            w_release(pd1)
        for i in range(4):
            ssc = stat_cols(4)
            ss = ssc.sub((slice(None), slice(0, 1)))
            act(tnh[:, :, :].re("p a c -> p (a c)"), hbuf[:, i, :], AF.Square, accum=ss)
            rs = rstd_from_ss(ss, 1, 1.0 / D)
            stt("dve", hbuf[:, i, :], hbuf[:, i, :], rs, gfin[:, :], ALU.mult, ALU.mult)
            r0 = tok0 + i * 128
            dma("sp", y_all[r0:r0 + 128, :], hbuf[:, i, :], f"y{i}")

    assert wstate["next"] == len(wseq)
    assert not plive, plive
    pr.emit(nc, es)
    print(f"[build] ops={len(pr.ops)} simulated_time_us={pr.sim_time / 1e3:.1f} busy_us=" +
          str({k: round(v / 1e3) for k, v in pr.sim_busy.items()}), flush=True)
    es.close()
    return nc, len(pr.ops)


_CACHE = {}


def kernel(**inputs):
    xp = np.asarray(inputs["x_prompt"], np.float32)
    xs = np.asarray(inputs["x_sample"], np.float32)
    bp, tp = xp.shape[0], xp.shape[1]
    bs, ts_ = xs.shape[0], xs.shape[1]
    npc, nsc = bp // NCORES, bs // NCORES
    seq_lens = [tp] * npc + [ts_] * nsc
    consts = _host_consts(inputs)
    key = tuple(seq_lens)
    if key not in _CACHE:
        _CACHE[key] = build_program(seq_lens)
    nc, _ = _CACHE[key]
    in_maps = []
    for c in range(NCORES):
        parts = [xp[c * npc + i] for i in range(npc)] + [xs[c * nsc + i] for i in range(nsc)]
        m = dict(consts)
        m["x_all"] = np.ascontiguousarray(np.concatenate(parts, 0))
        in_maps.append(m)
    res = run_bass_kernel_spmd(nc, in_maps, core_ids=list(range(NCORES)))
    yp = np.zeros_like(xp)
    ys = np.zeros_like(xs)
    for c in range(NCORES):
        y = res.results[c]["y_all"]
        o = 0
        for i in range(npc):
            yp[c * npc + i] = y[o:o + tp]
            o += tp
        for i in range(nsc):
            ys[c * nsc + i] = y[o:o + ts_]
            o += ts_
    return (yp, ys)
```
